# Optimizing a Trainium2 kernel written in Bass

```python
import jax, jax.numpy as jnp
from jax import lax
import numpy as np

D_MODEL = 1024
BATCH = 8
SEQ = 4096
DEPTH = 2

CHUNK = 64
HEAD_DIM = 64
D_MIX = D_MODEL
A_WIDTH = D_MIX // 4
R_WIDTH = 3 * D_MIX // 8
F_WIDTH = 3 * D_MIX // 8
A_GROUPS = A_WIDTH // HEAD_DIM
R_HEADS = R_WIDTH // HEAD_DIM
F_HEADS = F_WIDTH // HEAD_DIM
A_BLOCK = 128
Q_BLOCK = 128
ROPE_THETA = 10000.0
EPS = 1e-6

A_COLS = 3 * A_WIDTH
R_COLS = 4 * R_WIDTH
F_COLS = 4 * F_WIDTH + F_HEADS
D_IN = A_COLS + R_COLS + F_COLS

kernel_name = "hybrid_gmlp_retention_fox_block"


def _rms_norm(x, gain):
    xf = x.astype(jnp.float32)
    y = xf * lax.rsqrt(jnp.mean(xf * xf, axis=-1, keepdims=True) + EPS)
    return (y * gain.astype(jnp.float32)).astype(x.dtype)


def _layer_norm(x, gain=None):
    xf = x.astype(jnp.float32)
    mu = jnp.mean(xf, axis=-1, keepdims=True)
    var = jnp.mean(jnp.square(xf - mu), axis=-1, keepdims=True)
    y = (xf - mu) * lax.rsqrt(var + EPS)
    if gain is not None:
        y = y * gain.astype(jnp.float32)
    return y.astype(x.dtype)


def _rope(x):
    s, d = x.shape[1], x.shape[3]
    half = d // 2
    inv = ROPE_THETA ** (-jnp.arange(half, dtype=jnp.float32) / half)
    ang = jnp.arange(s, dtype=jnp.float32)[:, None] * inv[None, :]
    cos = jnp.cos(ang)[:, None, :].astype(x.dtype)
    sin = jnp.sin(ang)[:, None, :].astype(x.dtype)
    x1, x2 = x[..., :half], x[..., half:]
    return jnp.concatenate([x1 * cos - x2 * sin, x1 * sin + x2 * cos], axis=-1)


def _spatial_gating(u, v, ln_gain, w_s, b_s):
    b, s, _ = v.shape
    v = _layer_norm(v.reshape(b, s, A_GROUPS, HEAD_DIM), ln_gain)
    v = v.reshape(b, s // A_BLOCK, A_BLOCK, A_GROUPS, HEAD_DIM)
    pos = jnp.arange(A_BLOCK)
    allowed = (pos[None, :] // CHUNK) <= (pos[:, None] // CHUNK)
    w = jnp.where(allowed[None], w_s, jnp.zeros_like(w_s))
    mixed = jnp.einsum('gij,bnjgc->bnigc', w, v) + b_s.T[:, :, None]
    return u * mixed.reshape(b, s, A_WIDTH)


def _retention(q, k, v):
    b, s, h, d = q.shape
    nc = s // CHUNK
    dt = v.dtype
    q = _rope(q) * (d ** -0.5)
    k = _rope(k)
    log_gamma = jnp.log(1.0 - jnp.exp2(-5.0 - jnp.arange(h, dtype=jnp.float32)))
    pos = jnp.arange(CHUNK, dtype=jnp.float32)
    dist = jnp.abs(pos[:, None] - pos[None, :])
    intra_decay = jnp.exp(log_gamma[:, None, None] * dist).astype(dt)
    k_decay = jnp.exp(log_gamma[None, :] * (CHUNK - 1 - pos)[:, None]).astype(dt)
    q_decay = jnp.exp(log_gamma[None, :] * (pos + 1.0)[:, None]).astype(dt)
    chunk_decay = jnp.exp(log_gamma * CHUNK).astype(dt)
    qc = q.reshape(b, nc, CHUNK, h, d)
    kc = k.reshape(b, nc, CHUNK, h, d)
    vc = v.reshape(b, nc, CHUNK, h, d)
    scores = jnp.einsum('bnihd,bnjhd->bnhij', qc, kc) * intra_decay
    intra = jnp.einsum('bnhij,bnjhe->bnihe', scores, vc)
    kv = jnp.einsum('bnjhd,bnjhe->nbhde', kc * k_decay[:, :, None], vc)

    def step(state, kv_c):
        return state * chunk_decay[:, None, None] + kv_c, state

    _, s_prev = lax.scan(step, jnp.zeros((b, h, d, d), dt), kv)
    cross = jnp.einsum('bnihd,nbhde->bnihe', qc * q_decay[:, :, None], s_prev)
    out = (intra + cross).reshape(b, s, h, d)
    return _layer_norm(out)


def _forgetting_attention(q, k, v, f_logit):
    b, s, h, d = q.shape
    scale = d ** -0.5
    cum_f = jnp.cumsum(jax.nn.log_sigmoid(f_logit.astype(jnp.float32)), axis=1)
    cum_f = jnp.transpose(cum_f, (0, 2, 1))
    outs = []
    for i in range(s // Q_BLOCK):
        q0, q1 = i * Q_BLOCK, (i + 1) * Q_BLOCK
        qb = q[:, q0:q1]
        kb = k[:, :q1]
        vb = v[:, :q1]
        logits = jnp.einsum('bqhd,bkhd->bhqk', qb, kb).astype(jnp.float32) * scale
        logits = logits + cum_f[:, :, q0:q1, None] - cum_f[:, :, None, :q1]
        qpos = q0 + jnp.arange(Q_BLOCK)
        kpos = jnp.arange(q1)
        causal = kpos[None, :] <= qpos[:, None]
        logits = jnp.where(causal, logits, -jnp.inf)
        p = jax.nn.softmax(logits, axis=-1).astype(v.dtype)
        outs.append(jnp.einsum('bhqk,bkhd->bqhd', p, vb))
    return jnp.concatenate(outs, axis=1)


def _hybrid_layer(x, pre_g, post_g, w_in, b_f, a_ln_g, a_ws, a_bs, w_out):
    b, s, _ = x.shape
    h = _rms_norm(x, pre_g)
    z = jnp.einsum('bsd,de->bse', h, w_in)
    a_z, r_z, f_z = jnp.split(z, [A_COLS, A_COLS + R_COLS], axis=-1)

    a_u, a_v, a_g = jnp.split(a_z, 3, axis=-1)
    a_out = _spatial_gating(jax.nn.gelu(a_u), jax.nn.gelu(a_v), a_ln_g, a_ws, a_bs) * jax.nn.silu(a_g)

    r_q, r_k, r_v, r_g = jnp.split(r_z, 4, axis=-1)
    hs = (b, s, R_HEADS, HEAD_DIM)
    r_out = _retention(r_q.reshape(hs), r_k.reshape(hs), r_v.reshape(hs)).reshape(b, s, R_WIDTH)
    r_out = r_out * jax.nn.silu(r_g)

    f_q, f_k, f_v, f_g, f_lg = jnp.split(
        f_z, [F_WIDTH, 2 * F_WIDTH, 3 * F_WIDTH, 4 * F_WIDTH], axis=-1)
    hs = (b, s, F_HEADS, HEAD_DIM)
    f_out = _forgetting_attention(f_q.reshape(hs), f_k.reshape(hs), f_v.reshape(hs), f_lg + b_f)
    f_out = f_out.reshape(b, s, F_WIDTH) * jax.nn.silu(f_g)

    y = jnp.concatenate([a_out, r_out, f_out], axis=-1)
    o = jnp.einsum('bse,ed->bsd', y, w_out)
    return x + _rms_norm(o, post_g)


def setup_inputs(seed: int = 0) -> dict:
    key = jax.random.key(seed)
    ks = jax.random.split(key, 9)
    f32 = jnp.float32
    x = jax.random.normal(ks[0], (BATCH, SEQ, D_MODEL), f32)
    pre_gain = 1.0 + 0.02 * jax.random.normal(ks[1], (DEPTH, D_MODEL), f32)
    post_gain = 1.0 + 0.02 * jax.random.normal(ks[2], (DEPTH, D_MODEL), f32)
    w_in = jax.random.normal(ks[3], (DEPTH, D_MODEL, D_IN), f32) * (D_MODEL ** -0.5)
    b_forget = 3.0 + 0.5 * jax.random.normal(ks[4], (DEPTH, F_HEADS), f32)
    a_norm_gain = 1.0 + 0.02 * jax.random.normal(ks[5], (DEPTH, A_GROUPS, HEAD_DIM), f32)
    a_spatial_w = jax.random.normal(ks[6], (DEPTH, A_GROUPS, A_BLOCK, A_BLOCK), f32) * (A_BLOCK ** -0.5)
    a_spatial_b = 1.0 + 0.02 * jax.random.normal(ks[7], (DEPTH, A_GROUPS, A_BLOCK), f32)
    w_out = jax.random.normal(ks[8], (DEPTH, D_MIX, D_MODEL), f32) * (D_MIX ** -0.5)
    return {"x": x, "pre_gain": pre_gain, "post_gain": post_gain, "w_in": w_in,
            "b_forget": b_forget, "a_norm_gain": a_norm_gain, "a_spatial_w": a_spatial_w,
            "a_spatial_b": a_spatial_b, "w_out": w_out}


def reference(x, pre_gain, post_gain, w_in, b_forget, a_norm_gain, a_spatial_w, a_spatial_b, w_out):
    for l in range(DEPTH):
        x = _hybrid_layer(x, pre_gain[l], post_gain[l], w_in[l], b_forget[l],
                          a_norm_gain[l], a_spatial_w[l], a_spatial_b[l], w_out[l])
    return x
```

```python
import math
from contextlib import ExitStack

import numpy as np
import ml_dtypes
import concourse.bass as bass
import concourse.mybir as mybir
from concourse.bass_utils import run_bass_kernel_spmd

F32 = mybir.dt.float32
BF16 = mybir.dt.bfloat16
AF = mybir.ActivationFunctionType
ALU = mybir.AluOpType
AX = mybir.AxisListType

D = 1024
DIN = 3846
EPS = 1e-6
ENGS = ("pe", "act", "dve", "pool", "sp")

C_AU, C_AV, C_AG = 0, 256, 512
C_RQ, C_RK, C_RV, C_RG = 768, 1152, 1536, 1920
C_FQ, C_FK, C_FV, C_FG, C_FL = 2304, 2688, 3072, 3456, 3840

K_ID, K_TRI, K_SEL, K_QD, K_KD, K_CD = 0, 128, 256, 384, 768, 774
NCF = 774 + 192
KB_ID, KB_NEG, KB_RM = 0, 128, 256
NCB = 256 + 768


class Buf:
    __slots__ = ("name", "lw", "rd", "dsem", "excl")

    def __init__(self, name):
        self.name = name
        self.excl = name in ("pin0", "pin1", "ptr", "pmx", "pss0", "pss1", "pso0", "pso1")
        self.lw = None
        self.rd = {}
        self.dsem = None


class Sched:
    def __init__(self, nc, es):
        self.nc = nc
        self.es = es
        self.prog = {e: [] for e in ENGS}
        self.sems = {}
        self.cnt = {}
        self.seen = {e: {} for e in ENGS}
        self.capture = None
        for e in ENGS:
            self.sems[e] = es.enter_context(nc.semaphore("sem_" + e))
            self.cnt[e] = 0

    def _dsem(self, buf):
        if buf.dsem is None:
            key = "dsem_" + buf.name
            self.sems[key] = self.es.enter_context(self.nc.semaphore(key))
            self.cnt[key] = 0
            buf.dsem = key
        return buf.dsem

    def _wait(self, eng, ev):
        if ev is None:
            return
        key, val = ev
        if self.seen[eng].get(key, 0) >= val:
            return
        self.seen[eng][key] = val
        sem = self.sems[key]
        self.prog[eng].append(lambda E: E.wait_ge(sem, val))

    DEF_COST = {"pe": 0.3, "act": 0.6, "dve": 0.45, "pool": 0.8, "sp": 3.0}

    def op(self, eng, fns, reads=(), writes=(), owner=None, cost=None):
        if cost is None:
            cost = self.DEF_COST[eng]
        if self.capture is not None:
            self.capture.append((eng, fns, list(reads), list(writes), owner, cost))
            return
        if not isinstance(fns, (list, tuple)):
            fns = [fns]
        writes = list(writes) + [b for b in reads if b.excl and b not in writes]
        reads = [b for b in reads if not b.excl]
        for b in reads:
            self._wait(eng, b.lw)
        for b in writes:
            ev = b.lw
            if ev is not None and not (ev[0] == eng and eng == "pe"):
                self._wait(eng, ev)
            for k, v in b.rd.items():
                if k == eng and eng == "pe":
                    continue
                self._wait(eng, (k, v))
        if owner is None:
            key, inc = eng, 1
        else:
            key, inc = self._dsem(owner), 16
        self.cnt[key] += inc
        val = self.cnt[key]
        sem = self.sems[key]
        for f in fns[:-1]:
            self.prog[eng].append(f)
        last = fns[-1]
        self.prog[eng].append(lambda E: last(E).then_inc(sem, inc))
        ev = (key, val)
        for b in reads:
            if b.rd.get(key, 0) < val:
                b.rd[key] = val
        for b in writes:
            b.lw = ev
            b.rd = {}
        return ev

    def run_interleaved(self, chains):
        outer = self.capture
        lists = []
        for fn in chains:
            self.capture = []
            fn()
            lists.append(self.capture)
        self.capture = outer
        lists = [x for x in lists if x]
        idx = [0] * len(lists)
        left = sum(len(x) for x in lists)
        eng_free = {e: 0.0 for e in ENGS}
        tw, trd = {}, {}
        SYNC = 0.15

        def start_time(o):
            eng, fns, reads, writes, owner, cost = o
            t = eng_free[eng]
            for b_ in reads:
                if b_.excl:
                    t = max(t, tw.get(b_, 0.0) + SYNC, trd.get(b_, 0.0) + SYNC)
                else:
                    t = max(t, tw.get(b_, 0.0) + SYNC)
            for b_ in writes:
                t = max(t, tw.get(b_, 0.0) + SYNC, trd.get(b_, 0.0) + SYNC)
            return t

        while left:
            best, bt = None, None
            for c in range(len(lists)):
                if idx[c] < len(lists[c]):
                    t = start_time(lists[c][idx[c]])
                    rem = len(lists[c]) - idx[c]
                    key = (t, -rem)
                    if bt is None or key < bt:
                        best, bt = c, key
            o = lists[best][idx[best]]
            eng, fns, reads, writes, owner, cost = o
            t0 = bt[0]
            t1 = t0 + cost
            eng_free[eng] = t1 if owner is None else t0 + 0.1
            for b_ in reads:
                if b_.excl:
                    tw[b_] = max(tw.get(b_, 0.0), t1)
                else:
                    trd[b_] = max(trd.get(b_, 0.0), t1)
            for b_ in writes:
                tw[b_] = t1
                trd[b_] = 0.0
            self.op(*o)
            idx[best] += 1
            left -= 1

    def finish(self, eng="sp"):
        for key, val in self.cnt.items():
            if val > 0:
                self._wait(eng, (key, val))

    def emit(self, block):
        prog = self.prog

        @block.tensor
        def _(E):
            for f in prog["pe"]:
                f(E)

        @block.scalar
        def _(E):
            for f in prog["act"]:
                f(E)

        @block.vector
        def _(E):
            for f in prog["dve"]:
                f(E)

        @block.gpsimd
        def _(E):
            for f in prog["pool"]:
                f(E)

        @block.sync
        def _(E):
            for f in prog["sp"]:
                f(E)


class _Stop(Exception):
    pass


def build_program(S, L):
    import os
    KSTOP = int(os.environ.get("KSTOP", "0"))

    def stage(n):
        if KSTOP and n >= KSTOP:
            raise _Stop()

    NT = S // 128
    NS = S // 512
    nc = bass.Bass("TRN2", target_bir_lowering=False)

    def din(name, shape, dt=F32):
        return nc.dram_tensor(name, list(shape), dt, kind="ExternalInput").ap()

    x_d = din("x", [S, D])
    win_d = din("w_in", [L, D, DIN])
    wout_d = din("w_out", [L, D, D])
    pgT_d = din("pgT", [L, 128, 8])
    postg_d = din("postg", [L, 128, D])
    ang_d = din("ang", [L, 128, 256])
    asw_d = din("asw", [L, 4, 128, 128])
    asb_d = din("asb", [L, 128, 4])
    bfb_d = din("bfb", [L, 128, 6])
    cstf_d = din("cstf", [128, NCF])
    cstb_d = din("cstb", [128, NCB], BF16)
    cs_d = din("cs", [S, 128])
    smask_d = din("smask", [128, 128])
    out_d = nc.dram_tensor("out", [S, D], F32, kind="ExternalOutput").ap()
    xs_d = nc.dram_tensor("xscr", [S, D], F32, kind="Internal").ap() if L > 1 else None

    with ExitStack() as es:
        bufs = {}

        def B(name):
            if name not in bufs:
                bufs[name] = Buf(name)
            return bufs[name]

        def sb(name, shape, dt):
            return es.enter_context(nc.sbuf_tensor("s_" + name, list(shape), dt))

        def ps(name, shape, dt=F32):
            return es.enter_context(nc.psum_tensor("p_" + name, list(shape), dt))

        Wb = sb("Wb", [128, 8, DIN], BF16)
        Wo = sb("Wo", [128, 8, D], BF16)
        kT2 = sb("kT2", [128, 3, S], BF16)
        V = sb("V", [128, NT, 6, 66], BF16)
        NXA = 2
        xa = [sb(f"xa{i}", [128, D], F32) for i in range(NXA)]
        hT2 = [sb(f"hT{i}", [128, 8, 128], BF16) for i in range(2)]
        y2 = [sb(f"y{i}", [128, 4, D], BF16) for i in range(2)]
        xn = sb("xn", [128, D], BF16)
        cstf = sb("cstf", [128, NCF], F32)
        cstb = sb("cstb", [128, NCB], BF16)
        postg = sb("postg", [128, D], F32)
        ang = sb("ang", [128, 256], F32)
        asb = sb("asb", [128, 4], F32)
        bfb = sb("bfb", [128, 6], F32)
        pgT = sb("pgT", [128, 8], F32)
        WsT = sb("WsT", [128, 4, 128], BF16)
        cst = [sb(f"cs{i}", [128, 128], F32) for i in range(2)]
        sc = [sb(f"sc{i}", [128, 512], F32) for i in range(4)]
        rsA = sb("rsA", [128, 384], F32)
        rsB = sb("rsB", [128, 384], F32)
        rsE = sb("rsE", [128, 384], F32)
        vln = sb("vln", [128, 256], BF16)
        qr = sb("qr", [128, 384], BF16)
        kr = sb("kr", [128, 384], BF16)
        kd = sb("kd", [128, 384], BF16)
        qdT = sb("qdT", [128, 3, 128], BF16)
        kT = sb("kT", [128, 3, 128], BF16)
        vr = sb("vr", [128, 384], BF16)
        PsT = sb("PsT", [128, 6, 128], BF16)
        Sst = sb("Sst", [128, 3, 64], F32)
        Sbf = sb("Sbf", [128, 3, 64], BF16)
        sgf2 = [sb(f"sgf{i}", [128, 4, 384], BF16) for i in range(2)]
        ncum = sb("ncum", [128, NT, 6], F32)
        nbI = sb("nbI", [128, NT, 6], F32)
        qT2b = [sb(f"qT2{i}", [128, 3, 512], BF16) for i in range(2)]
        NPT = 2
        PT = [sb(f"PT{i}", [128, 512], BF16) for i in range(NPT)]
        st = sb("st", [128, 64], F32)
        qz = sb("qz", [128, 2, 512], BF16)
        qTz = sb("qTz", [128, 6, 128], BF16)

        pin = [ps(f"pin{i}", [128, 512]) for i in range(2)]
        ptr = ps("ptr", [128, 8, 128], BF16)
        pmx = ps("pmx", [128, 512])
        pss = [ps(f"pss{i}", [128, 512]) for i in range(2)]
        pso = [ps(f"pso{i}", [128, 4, 128]) for i in range(2)]

        Sd = Sched(nc, es)
        block = es.enter_context(nc.Block())
        pso1_b = pso[1][:].rearrange("p a b -> p (a b)").bitcast(BF16).rearrange("p (a b) -> p a b", a=8)
        ptr_f = ptr[:].rearrange("p a b -> p (a b)").bitcast(F32)
        xn_f = xn[:].bitcast(F32)
        pss0_b = pss[0][:].bitcast(BF16).rearrange("p (a b) -> p a b", a=8)
        pso0_f = pso[0][:].rearrange("p a b -> p (a b)")
        PT0v = PT[0][:].rearrange("p (a b) -> p a b", a=4)
        PT1v = PT[1][:].rearrange("p (a b) -> p a b", a=4)
        op = Sd.op

        ident_f = cstf[:, K_ID:K_ID + 128]
        tri = cstf[:, K_TRI:K_TRI + 128]
        sel = cstf[:, K_SEL:K_SEL + 128]
        qdec = cstf[:, K_QD:K_QD + 384].rearrange("p (a b) -> p a b", a=3)
        kdec = cstf[:, K_KD:K_KD + 6]
        cd2t = cstf[:, K_CD:K_CD + 192].rearrange("p (a b) -> p a b", a=3)
        ident_b = cstb[:, KB_ID:KB_ID + 128]
        negm = cstb[:, KB_NEG:KB_NEG + 128]
        rmask = cstb[:, KB_RM:KB_RM + 768].rearrange("p (a b) -> p a b", a=6)

        def fsz(ap):
            n = 1
            for d in ap.shape[1:]:
                n *= d
            return n

        def dma(out, in_, reads, writes, owner, eng="sp"):
            op(eng, lambda E: E.dma_start(out=out, in_=in_), reads, writes, owner=owner, cost=3.0)

        def act(out, in_, func, reads, writes, bias=0.0, scale=1.0, accum=None):
            c = 0.25 + fsz(out) / 1200.0 + (0.1 if accum is not None else 0.0)
            if accum is None:
                op("act", lambda E: E.activation(out=out, in_=in_, func=func, bias=bias, scale=scale), reads, writes, cost=c)
            else:
                op("act", lambda E: E.activation(out=out, in_=in_, func=func, bias=bias, scale=scale, accum_out=accum), reads, writes, cost=c)

        def vcost(eng, out, two_src):
            n = fsz(out)
            if eng == "pool":
                return 0.35 + n / 700.0
            return 0.16 + n / (960.0 if two_src else 1600.0)

        def tt(eng, out, in0, in1, alu, reads, writes):
            op(eng, lambda E: E.tensor_tensor(out=out, in0=in0, in1=in1, op=alu), reads, writes, cost=vcost(eng, out, True))

        def ts(eng, out, in0, s1, s2, op0, op1, reads, writes):
            c = vcost(eng, out, False)
            if op1 is None:
                op(eng, lambda E: E.tensor_scalar(out=out, in0=in0, scalar1=s1, scalar2=None, op0=op0), reads, writes, cost=c)
            else:
                op(eng, lambda E: E.tensor_scalar(out=out, in0=in0, scalar1=s1, scalar2=s2, op0=op0, op1=op1), reads, writes, cost=c)

        def stt(eng, out, in0, scalar, in1, op0, op1, reads, writes):
            op(eng, lambda E: E.scalar_tensor_tensor(out=out, in0=in0, scalar=scalar, in1=in1, op0=op0, op1=op1), reads, writes,
               cost=vcost(eng, out, True))

        def cp(eng, out, in_, reads, writes):
            if eng == "act":
                op("act", lambda E: E.copy(out=out, in_=in_), reads, writes, cost=0.25 + fsz(out) / 1200.0)
            else:
                op(eng, lambda E: E.tensor_copy(out=out, in_=in_), reads, writes, cost=vcost(eng, out, False))

        def mm(specs, reads, writes):
            fns = []
            c = 0.0
            for sp_ in specs:
                (o, l, r, s0, s1) = sp_[:5]
                sk = len(sp_) > 5
                c += 0.04 + max(64, fsz(o)) / 1600.0
                fns.append(lambda E, o=o, l=l, r=r, s0=s0, s1=s1, sk=sk: E.matmul(o, lhsT=l, rhs=r, start=s0, stop=s1, skip_group_check=sk))
            op("pe", fns, reads, writes, cost=c)

        def tr(specs, reads, writes):
            fns = []
            for (o, i, idn) in specs:
                fns.append(lambda E, o=o, i=i, idn=idn: E.transpose(out=o, in_=i, identity=idn))
            op("pe", fns, reads, writes, cost=0.1 * len(specs))

        def silu_chain(src_ps, src_buf, tmp, tmp_buf, out, out_buf, n_extra_reads=()):
            act(tmp, src_ps, AF.Exp, [src_buf], [tmp_buf], scale=-1.0)
            act(tmp, tmp, AF.Ln, [tmp_buf], [tmp_buf], bias=1.0)
            act(tmp, tmp, AF.Exp, [tmp_buf], [tmp_buf], scale=-1.0)
            tt("dve", out, src_ps, tmp, ALU.mult, [src_buf, tmp_buf], [out_buf])

        def rstd_small(var_ap, out_ap, buf, scale=1.0):
            act(out_ap, var_ap, AF.Ln, [buf], [buf], bias=EPS, scale=scale)
            act(out_ap, out_ap, AF.Exp, [buf], [buf], scale=-0.5)

        dma(cstf[:], cstf_d[:, :], [], [B("cstf")], B("cstf"))
        dma(cstb[:], cstb_d[:, :], [], [B("cstb")], B("cstb"))
        op("pool", lambda E: E.memset(V[:, :, :, 64:66], 1.0), [], [B("Vones")])
        op("pool", lambda E: E.memset(qz[:], 0.0), [], [B("qz")])
        op("pool", lambda E: E.memset(qTz[:], 0.0), [], [B("qTz")])

        xa_i = [0]

        xa_mode = ["both"]

        def next_xa():
            if xa_mode[0] == "y":
                return xa[0], B("xa0")
            i = xa_i[0] % NXA
            xa_i[0] += 1
            return xa[i], B(f"xa{i}")


        try:
          stage(1)
          for l in range(L):
              src_d = x_d if l == 0 else xs_d
              dst_d = out_d if l == L - 1 else xs_d
              srcB = B("xsrc") if l == 0 else B("xscr")
              dstB = B("out") if l == L - 1 else B("xscr")

              def setup_early(l):
                  dma(pgT[:], pgT_d[l], [], [B("pgT")], B("pgT"))
                  dma(ang[:], ang_d[l], [], [B("ang")], B("ang"))
                  dma(asb[:], asb_d[l], [], [B("asb")], B("asb"))
                  dma(bfb[:], bfb_d[l], [], [B("bfb")], B("bfb"))
                  ci = 0
                  for kc in range(8):
                      for c0 in range(0, DIN, 1024):
                          c1 = min(DIN, c0 + 1024)
                          stg, stgB = next_xa()
                          dma(stg[:, 0:c1 - c0], win_d[l, kc * 128:(kc + 1) * 128, c0:c1], [], [stgB], stgB)
                          if ci % 2 == 0:
                              ts("dve", Wb[:, kc, c0:c1], stg[:, 0:c1 - c0], pgT[:, kc:kc + 1], None, ALU.mult, None,
                                 [stgB, B("pgT")], [B("Wb")])
                          else:
                              act(Wb[:, kc, c0:c1], stg[:, 0:c1 - c0], AF.Copy, [stgB, B("pgT")], [B("Wb")], scale=pgT[:, kc:kc + 1])
                          ci += 1
                  stg, stgB = next_xa()
                  stg4 = stg[:, 0:512].rearrange("p (g j) -> p g j", g=4)
                  dma(stg4, asw_d[l].rearrange("g i j -> i g j"), [], [stgB], stgB)
                  pm4 = pmx[:, 0:512].rearrange("p (g i) -> p g i", g=4)
                  tr([(pm4[:, g, :], stg4[:, g, :], ident_f) for g in range(4)], [stgB, B("cstf")], [B("pmx")])
                  stg2, stg2B = next_xa()
                  dma(stg2[:, 0:128], smask_d[:, :], [], [stg2B], stg2B)
                  tt("dve", WsT[:], pm4, stg2[:, 0:128].unsqueeze(1).broadcast_to([128, 4, 128]), ALU.mult,
                     [B("pmx"), stg2B], [B("WsT")])

              def setup_late(l):
                  dma(postg[:], postg_d[l], [], [B("postg")], B("postg"))
                  for kc in range(8):
                      stg, stgB = next_xa()
                      dma(stg[:], wout_d[l, kc * 128:(kc + 1) * 128, :], [], [stgB], stgB)
                      cp(["dve", "act"][kc % 2], Wo[:, kc, :], stg[:], [stgB], [B("Wo")])

              if l == 0:
                  setup_early(0)
                  setup_late(0)
              stage(2)
              stage(3)
              op("pool", lambda E: E.memset(Sst[:], 0.0), [], [B("Sst")])
              op("pool", lambda E: E.memset(Sbf[:], 0.0), [], [B("Sbf")])

              def pre(t):
                  tl = t % 4
                  xt, xtB = xa[0], B("xa0")
                  dma(xt[:], src_d[t * 128:(t + 1) * 128, :], [srcB], [xtB], xtB)
                  cs_t, csB = cst[t % 2], B(f"cs{t % 2}")
                  dma(cs_t[:], cs_d[t * 128:(t + 1) * 128, :], [], [csB], csB)
                  act(xn[:], xt[:], AF.Square, [xtB], [B("xn"), B("st_ss")], accum=st[:, 0:1])
                  rstd_small(st[:, 0:1], st[:, 1:2], B("st_ss"), scale=1.0 / D)
                  ts("dve", xn[:], xt[:], st[:, 1:2], None, ALU.mult, None, [xtB, B("st_ss")], [B("xn")])
                  tr([(ptr[:, kc, :], xn[:, kc * 128:(kc + 1) * 128], ident_b) for kc in range(8)],
                     [B("xn"), B("cstb")], [B("ptr")])
                  hB = B(f"hT{t % 2}")
                  cp("dve", hT2[t % 2][:], ptr[:], [B("ptr")], [hB])


              def chA(t):
                  tl = t % 4
                  hB = B(f"hT{t % 2}")
                  Ib = (t // 4) % 2
                  cs_t, csB = cst[t % 2], B(f"cs{t % 2}")
                  pin_i = [0]

                  def inproj(c0, n):
                      k = 0
                      pin_i[0] += 1
                      pb, pbB = pin[k], B(f"pin{k}")
                      mm([(pb[:, 0:n], hT2[t % 2][:, kc, :], Wb[:, kc, c0:c0 + n], kc == 0, kc == 7)
                          for kc in range(8)], [hB, B("Wb")], [pbB])
                      return pb, pbB

                  pa, paB = inproj(C_AU, 512)
                  cA, cAB = sc[0], B("sc0")
                  eA, eAB = sc[1], B("sc1")
                  guv, guvB = sc[2], B("sc2")
                  act(cA[:], pa[:], AF.Square, [paB], [cAB])
                  ts("pool", cA[:], cA[:], 0.044715, 1.0, ALU.mult, ALU.add, [cAB], [cAB])
                  tt("dve", cA[:], cA[:], pa[:], ALU.mult, [cAB, paB], [cAB])
                  act(eA[:], cA[:], AF.Exp, [cAB], [eAB], scale=-2.0 * math.sqrt(2.0 / math.pi))
                  act(eA[:], eA[:], AF.Ln, [eAB], [eAB], bias=1.0)
                  act(eA[:], eA[:], AF.Exp, [eAB], [eAB], scale=-1.0)
                  tt("dve", guv[:], pa[:], eA[:], ALU.mult, [paB, eAB], [guvB])
                  gv = guv[:, 256:512]
                  gv3 = gv.rearrange("p (g c) -> p g c", g=4)
                  sq3 = cA[:, 0:256].rearrange("p (g c) -> p g c", g=4)
                  stA = B("st_A")
                  op("dve", lambda E: E.tensor_reduce(out=st[:, 8:12], in_=gv3, axis=AX.X, op=ALU.add), [guvB], [stA])
                  tt("pool", cA[:, 0:256], gv, gv, ALU.mult, [guvB], [cAB])
                  op("dve", lambda E: E.tensor_reduce(out=st[:, 12:16], in_=sq3, axis=AX.X, op=ALU.add), [cAB], [stA])
                  ts("dve", st[:, 8:12], st[:, 8:12], 1.0 / 64, None, ALU.mult, None, [stA], [stA])
                  tt("dve", st[:, 16:20], st[:, 8:12], st[:, 8:12], ALU.mult, [stA], [stA])
                  stt("dve", st[:, 12:16], st[:, 12:16], 1.0 / 64, st[:, 16:20], ALU.mult, ALU.subtract, [stA], [stA])
                  rstd_small(st[:, 12:16], st[:, 12:16], stA)
                  vc, vcB = sc[3][:, 0:256], B("sc3a")
                  vc3 = vc.rearrange("p (g c) -> p g c", g=4)
                  tt("dve", vc3, gv3, st[:, 8:12].unsqueeze(2).broadcast_to([128, 4, 64]), ALU.subtract, [guvB, stA], [vcB])
                  tt("dve", vc3, vc3, st[:, 12:16].unsqueeze(2).broadcast_to([128, 4, 64]), ALU.mult, [vcB, stA], [vcB])
                  tt("pool", vln[:], vc, ang[:], ALU.mult, [vcB, B("ang")], [B("vln")])
                  pm = pmx[:, 0:256]
                  pm3 = pm.rearrange("p (g c) -> p g c", g=4)
                  mm([(pmx[:, g * 64:(g + 1) * 64], WsT[:, g, :], vln[:, g * 64:(g + 1) * 64], True, True) for g in range(4)],
                     [B("WsT"), B("vln")], [B("pmx")])
                  mb, mbB = sc[3][:, 256:512], B("sc3b")
                  mb3 = mb.rearrange("p (g c) -> p g c", g=4)
                  tt("dve", mb3, pm3, asb[:, 0:4].unsqueeze(2).broadcast_to([128, 4, 64]), ALU.add, [B("pmx"), B("asb")], [mbB])
                  tt("pool", mb, mb, guv[:, 0:256], ALU.mult, [mbB, guvB], [mbB])
                  pg_, pgB = inproj(C_AG, 256)
                  e2, e2B = sc[1][:, 0:256], B("sc1")
                  silu_chain(pg_[:, 0:256], pgB, e2, e2B, e2, e2B)
                  tt("pool", y2[Ib][:, tl, 0:256], mb, e2, ALU.mult, [mbB, e2B], [B(f"yA{Ib}_{tl}")])


              def chR(t):
                  tl = t % 4
                  hB = B(f"hT{t % 2}")
                  Ib = (t // 4) % 2
                  cs_t, csB = cst[t % 2], B(f"cs{t % 2}")
                  pin_i = [0]

                  def inproj(c0, n):
                      k = 1
                      pin_i[0] += 1
                      pb, pbB = pin[k], B(f"pin{k}")
                      mm([(pb[:, 0:n], hT2[t % 2][:, kc, :], Wb[:, kc, c0:c0 + n], kc == 0, kc == 7)
                          for kc in range(8)], [hB, B("Wb")], [pbB])
                      return pb, pbB

                  cos2 = cs_t[:, 0:64].unsqueeze(1).broadcast_to([128, 6, 64])
                  sin_b = cs_t[:, 64:96].unsqueeze(1).broadcast_to([128, 6, 32])
                  nsin_b = cs_t[:, 96:128].unsqueeze(1).broadcast_to([128, 6, 32])
                  tA, tAB = rsA, B("rsA")
                  tB_, tBB = rsB, B("rsB")
                  for (c0, dst, dstB) in ((C_RQ, qr, B("qr")), (C_RK, kr, B("kr"))):
                      pq, pqB = inproj(c0, 384)
                      pq3 = pq[:, 0:384].rearrange("p (h c) -> p h c", h=6)
                      pq4 = pq[:, 0:384].rearrange("p (h s f) -> p h s f", h=6, s=2)
                      tA3 = tA[:, 0:384].rearrange("p (h c) -> p h c", h=6)
                      tB4 = tB_[:, 0:384].rearrange("p (h s f) -> p h s f", h=6, s=2)
                      tt("dve", tA3, pq3, cos2, ALU.mult, [pqB, csB], [tAB])
                      tt("dve", tB4[:, :, 0, :], pq4[:, :, 1, :], nsin_b, ALU.mult, [pqB, csB], [tBB])
                      tt("dve", tB4[:, :, 1, :], pq4[:, :, 0, :], sin_b, ALU.mult, [pqB, csB], [tBB])
                      tt("dve", dst[:], tA[:, 0:384], tB_[:, 0:384], ALU.add, [tAB, tBB], [dstB])
                  tt("pool", kd[:].rearrange("p (h c) -> p h c", h=6), kr[:].rearrange("p (h c) -> p h c", h=6),
                     kdec.unsqueeze(2).broadcast_to([128, 6, 64]), ALU.mult, [B("kr"), B("cstf")], [B("kd")])
                  tr([(pso1_b[:, p, :], qr[:, p * 128:(p + 1) * 128], ident_b) for p in range(3)]
                     + [(pso1_b[:, 3 + p, :], kr[:, p * 128:(p + 1) * 128], ident_b) for p in range(3)],
                     [B("qr"), B("kr"), B("cstb")], [B("pso1")])
                  cp("dve", qTz[0:64, 0:6:2, :], pso1_b[0:64, 0:3, :], [B("pso1")], [B("qTz")])
                  cp("dve", qTz[64:128, 1:6:2, :], pso1_b[64:128, 0:3, :], [B("pso1")], [B("qTz")])
                  tt("dve", qdT[:], pso1_b[:, 0:3, :], qdec, ALU.mult, [B("pso1"), B("cstf")], [B("qdT")])
                  cp("act", kT[:], pso1_b[:, 3:6, :], [B("pso1")], [B("kT")])
                  pv, pvB = inproj(C_RV, 384)
                  cp("act", vr[:], pv[:, 0:384], [pvB], [B("vr")])
                  pgr, pgrB = inproj(C_RG, 384)
                  eR, eRB = rsE[:, 0:384], B("rsE")
                  silu_chain(pgr[:, 0:384], pgrB, eR, eRB, eR, eRB, ())
                  specs0, specs1 = [], []
                  for h in range(6):
                      p_, s_ = h // 2, h % 2
                      rows = slice(s_ * 64, (s_ + 1) * 64)
                      if s_ == 0:
                          specs0.append((pso[1][:].rearrange("p a b -> p (a b)")[:, p_ * 128:(p_ + 1) * 128], kT[:, p_, :], qTz[:, h, :], True, True))
                      else:
                          specs1.append((pin[1][:, p_ * 128:(p_ + 1) * 128], kT[:, p_, :], qTz[:, h, :], True, True))
                  mm(specs0, [B("kT"), B("qTz")], [B("pso1")])
                  mm(specs1, [B("kT"), B("qTz")], [B("pin1")])
                  tt("dve", PsT[:, 0:6:2, :], pso[1][:, 0:3, :], rmask[:, 0:6:2, :], ALU.mult,
                     [B("pso1"), B("cstb")], [B("PsT0")])
                  tt("dve", PsT[:, 1:6:2, :], pin[1][:, 0:384].rearrange("p (h i) -> p h i", h=3), rmask[:, 1:6:2, :], ALU.mult,
                     [B("pin1"), B("cstb")], [B("PsT1")])
                  pro = pso[1][:].rearrange("p a b -> p (a b)")
                  specs = []
                  for h in range(6):
                      p_, s_ = h // 2, h % 2
                      rows = slice(s_ * 64, (s_ + 1) * 64)
                      specs.append((pro[:, h * 64:(h + 1) * 64], PsT[:, h, :], vr[:, h * 64:(h + 1) * 64], True, False))
                      specs.append((pro[:, h * 64:(h + 1) * 64], qdT[rows, p_, :], Sbf[rows, p_, :], False, True))
                  mm(specs, [B("PsT0"), B("PsT1"), B("vr"), B("qdT"), B("Sbf")], [B("pso1")])
                  pkv = pin[1][:, 0:384].rearrange("p (a b) -> p a b", a=3)
                  mm([(pin[1][:, p_ * 128:(p_ + 1) * 128], kd[:, p_ * 128:(p_ + 1) * 128], vr[:, p_ * 128:(p_ + 1) * 128], True, True) for p_ in range(3)],
                     [B("kd"), B("vr")], [B("pin1")])
                  tt("pool", Sst[:], Sst[:], cd2t, ALU.mult, [B("Sst"), B("cstf")], [B("Sst")])
                  tt("dve", Sst[0:64], Sst[0:64], pkv[0:64, :, 0:64], ALU.add, [B("Sst"), B("pin1")], [B("Sst")])
                  tt("dve", Sst[64:128], Sst[64:128], pkv[64:128, :, 64:128], ALU.add, [B("Sst"), B("pin1")], [B("Sst")])
                  cp("pool", Sbf[:], Sst[:], [B("Sst")], [B("Sbf")])
                  ro3 = pro[:, 0:384].rearrange("p (h c) -> p h c", h=6)
                  stR = B("st_R")
                  sqR, sqRB = rsA, B("rsA")
                  oc, ocB = rsB, B("rsB")
                  oc3 = oc[:, 0:384].rearrange("p (h c) -> p h c", h=6)
                  op("dve", lambda E: E.tensor_reduce(out=st[:, 24:30], in_=ro3, axis=AX.X, op=ALU.add), [B("pso1")], [stR])
                  act(sqR[:, 0:384], pro[:, 0:384], AF.Square, [B("pso1")], [sqRB])
                  op("dve", lambda E: E.tensor_reduce(out=st[:, 30:36], in_=sqR[:, 0:384].rearrange("p (h c) -> p h c", h=6),
                                                      axis=AX.X, op=ALU.add), [sqRB], [stR])
                  ts("dve", st[:, 24:30], st[:, 24:30], 1.0 / 64, None, ALU.mult, None, [stR], [stR])
                  tt("dve", st[:, 36:42], st[:, 24:30], st[:, 24:30], ALU.mult, [stR], [stR])
                  stt("dve", st[:, 30:36], st[:, 30:36], 1.0 / 64, st[:, 36:42], ALU.mult, ALU.subtract, [stR], [stR])
                  rstd_small(st[:, 30:36], st[:, 30:36], stR)
                  tt("dve", oc3, ro3, st[:, 24:30].unsqueeze(2).broadcast_to([128, 6, 64]), ALU.subtract, [B("pso1"), stR], [ocB])
                  tt("dve", oc3, oc3, st[:, 30:36].unsqueeze(2).broadcast_to([128, 6, 64]), ALU.mult, [ocB, stR], [ocB])
                  tt("pool", y2[Ib][:, tl, 256:640], oc[:, 0:384], eR, ALU.mult, [ocB, eRB], [B(f"yR{Ib}_{tl}")])


              def chF(t):
                  tl = t % 4
                  hB = B(f"hT{t % 2}")
                  Ib = (t // 4) % 2
                  cs_t, csB = cst[t % 2], B(f"cs{t % 2}")
                  pin_i = [0]

                  def inproj(c0, n):
                      pin_i[0] += 1
                      pb, pbB = ptr_f, B("ptr")
                      mm([(pb[:, 0:n], hT2[t % 2][:, kc, :], Wb[:, kc, c0:c0 + n], kc == 0, kc == 7)
                          for kc in range(8)], [hB, B("Wb")], [pbB])
                      return pb, pbB

                  pfv, pfvB = inproj(C_FV, 384)
                  cp("act", V[:, t, :, 0:64], pfv[:, 0:384].rearrange("p (h c) -> p h c", h=6), [pfvB], [B(f"V{t}")])
                  pfg, pfgB = inproj(C_FG, 390)
                  eF, eFB = xn_f[:, 0:384], B("xn")
                  silu_chain(pfg[:, 0:384], pfgB, eF, eFB, sgf2[Ib][:, tl, :], B(f"sgf{Ib}_{tl}"))
                  stF = B("st_F")
                  tt("dve", st[:, 44:50], pfg[:, 384:390], bfb[:], ALU.add, [pfgB, B("bfb")], [stF])
                  act(st[:, 44:50], st[:, 44:50], AF.Exp, [stF], [stF], scale=-1.0)
                  act(st[:, 44:50], st[:, 44:50], AF.Ln, [stF], [stF], bias=1.0)
                  if t == 0:
                      mm([(pmx[:, 256:262], tri, st[:, 44:50], True, True)], [B("cstf"), stF], [B("pmx")])
                  else:
                      mm([(pmx[:, 256:262], tri, st[:, 44:50], True, False),
                          (pmx[:, 256:262], sel, ncum[:, t - 1, :], False, True)],
                         [B("cstf"), stF, B(f"ncum{t - 1}")], [B("pmx")])
                  cp("dve", ncum[:, t, :], pmx[:, 256:262], [B("pmx")], [B(f"ncum{t}")])
                  for (cb_, isq) in ((C_FQ, True), (C_FK, False)):
                      specs = []
                      for p_ in range(3):
                          for kc in range(8):
                              specs.append((ptr_f[:, p_ * 128:(p_ + 1) * 128], Wb[:, kc, cb_ + p_ * 128:cb_ + (p_ + 1) * 128],
                                            hT2[t % 2][:, kc, :], kc == 0, kc == 7))
                      mm(specs, [hB, B("Wb")], [B("ptr")])
                      src3 = ptr_f[:, 0:384].rearrange("p (a b) -> p a b", a=3)
                      if isq:
                          cp("act", qT2b[Ib][:, :, tl * 128:(tl + 1) * 128], src3, [B("ptr")], [B(f"qT2_{Ib}")])
                      else:
                          cp("dve", kT2[:, :, t * 128:(t + 1) * 128], src3, [B("ptr")], [B(f"kT2_{t}")])

              def FoX(I):
                  Ib = I % 2
                  nj = 4 * I + 4
                  mm([(pss[0][:, 0:6], sel, ncum[:, 4 * I + 1, :], True, True)], [B("cstf"), B(f"ncum{4 * I + 1}")], [B("pss0")])
                  tt("dve", nbI[:, 0:nj, :], ncum[:, 0:nj, :], pss[0][:, 0:6].unsqueeze(1).broadcast_to([128, nj, 6]), ALU.subtract,
                     [B(f"ncum{j}") for j in range(nj)] + [B("pss0")], [B("nbI")])
                  steps = [(h, j) for h in range(6) for j in range(nj)]

                  def fox_S(idx):
                      h, j = steps[idx]
                      p_, s_ = h // 2, h % 2
                      rows = slice(s_ * 64, (s_ + 1) * 64)
                      a = j - 4 * I
                      lo = max(a, 0) * 128
                      psb, psB = pss[idx % 2], B(f"pss{idx % 2}")
                      kslc = kT2[:, p_, j * 128:(j + 1) * 128]
                      if s_ == 0 and j == 0:
                          cp("dve", qz[0:64, 0, :], qT2b[Ib][0:64, p_, :], [B(f"qT2_{Ib}")], [B("qz")])
                          cp("dve", qz[64:128, 1, :], qT2b[Ib][64:128, p_, :], [B(f"qT2_{Ib}")], [B("qz")])
                      if a < 0:
                          specs = [(psb[:, 0:512], kslc, qz[:, s_, 0:512], True, True)]
                      else:
                          specs = [(psb[:, lo:lo + 128], ident_b, negm, True, False),
                                   (psb[:, lo:lo + 128], kslc, qz[:, s_, lo:lo + 128], False, True)]
                          if lo + 128 < 512:
                              specs.append((psb[:, lo + 128:512], kslc, qz[:, s_, lo + 128:512], True, True))
                      mm(specs, [B(f"kT2_{j}"), B("qz"), B("cstb")], [psB])

                  def fox_PV(idx):
                      h, j = steps[idx]
                      a = j - 4 * I
                      lo = max(a, 0) * 128
                      po, poB = pso[0], B("pso0")
                      psb, psB = pss[idx % 2], B(f"pss{idx % 2}")
                      ptb, ptB = PT[idx % NPT], B(f"PT{idx % NPT}")
                      act(ptb[:, lo:512], psb[:, lo:512], AF.Exp, [psB, B("nbI")], [ptB], bias=nbI[:, j, h:h + 1], scale=0.125)
                      specs = []
                      for qt in range(lo // 128, 4):
                          specs.append((po[:, qt, 0:65], ptb[:, qt * 128:(qt + 1) * 128], V[:, j, h, 0:65], j == 0 and qt == 0, j == 4 * I + qt, True))
                      mm(specs, [ptB, B(f"V{j}"), B("Vones")], [poB])
                      if j == nj - 1:
                          op("dve", lambda E, po=po: E.reciprocal(out=st[:, 52:56], in_=po[:, :, 64:65].rearrange("p a b -> p (a b)")),
                             [poB], [B("st_O")])
                          for qt in range(4):
                              stt("dve", y2[Ib][:, qt, 640 + h * 64:640 + (h + 1) * 64], po[:, qt, 0:64], st[:, 52 + qt:53 + qt],
                                  sgf2[Ib][:, qt, h * 64:(h + 1) * 64], ALU.mult, ALU.mult,
                                  [poB, B("st_O"), B(f"sgf{Ib}_{qt}")], [B(f"yF{Ib}_{h}")])

                  fox_S(0)
                  for idx in range(len(steps)):
                      if idx + 1 < len(steps):
                          fox_S(idx + 1)
                      fox_PV(idx)

              def P3t(t):
                  tl = t % 4
                  Ib = (t // 4) % 2
                  ob = [(pss[1][:], B("pss1")), (pso0_f, B("pso0"))]
                  yreads = [B(f"yA{Ib}_{tl}"), B(f"yR{Ib}_{tl}")] + [B(f"yF{Ib}_{h}") for h in range(6)]
                  tr([(pss0_b[:, kc, :], y2[Ib][:, tl, kc * 128:(kc + 1) * 128], ident_b) for kc in range(8)], yreads + [B("cstb")], [B("pss0")])
                  cp("act", PT0v, pss0_b[:, 0:4, :], [B("pss0")], [B("PT0")])
                  cp("dve", PT1v, pss0_b[:, 4:8, :], [B("pss0")], [B("PT1")])
                  specs = []
                  for kc in range(8):
                      lh = (PT0v if kc < 4 else PT1v)[:, kc % 4, :]
                      for hf in range(2):
                          specs.append((ob[hf][0], lh, Wo[:, kc, hf * 512:(hf + 1) * 512], kc == 0, kc == 7))
                  mm(specs, [B("PT0"), B("PT1"), B("Wo")], [ob[0][1], ob[1][1]])
                  xr, xrB = xa[1], B("xa1")
                  dma(xr[:], src_d[t * 128:(t + 1) * 128, :], [srcB], [xrB], xrB)
                  stP = B("st_P")
                  for hf in range(2):
                      act(pss[0][:], ob[hf][0], AF.Square, [ob[hf][1]], [B("pss0"), stP], accum=st[:, 58 + hf:59 + hf])
                      tt("dve", ob[hf][0], ob[hf][0], postg[:, hf * 512:(hf + 1) * 512], ALU.mult, [ob[hf][1], B("postg")], [ob[hf][1]])
                  tt("dve", st[:, 60:61], st[:, 58:59], st[:, 59:60], ALU.add, [stP], [stP])
                  rstd_small(st[:, 60:61], st[:, 60:61], stP, scale=1.0 / D)
                  for hf in range(2):
                      stt("dve", xr[:, hf * 512:(hf + 1) * 512], ob[hf][0], st[:, 60:61], xr[:, hf * 512:(hf + 1) * 512],
                          ALU.mult, ALU.add, [ob[hf][1], stP, xrB], [xrB])
                  dma(dst_d[t * 128:(t + 1) * 128, :], xr[:], [xrB], [dstB], xrB)

              def P1(I):
                  if I == 0:
                      pre(0)
                  for tl_ in range(4):
                      t_ = 4 * I + tl_

                      def c_a(t_=t_):
                          chA(t_)

                      def c_r(t_=t_):
                          chR(t_)

                      def c_fp(t_=t_, tl_=tl_):
                          chF(t_)
                          if t_ + 1 < NT:
                              pre(t_ + 1)

                      Sd.run_interleaved([c_a, c_r, c_fp])

              def P3(I):
                  for tl_ in range(4):
                      P3t(4 * I + tl_)

              stage(8)
              P1(0)
              for I in range(NS):
                  def side_x(I=I):
                      FoX(I)
                      P3(I)

                  def side_y(I=I):
                      if I + 1 < NS:
                          P1(I + 1)
                      elif l + 1 < L:
                          xa_mode[0] = "y"
                          setup_early(l + 1)
                          xa_mode[0] = "both"

                  Sd.run_interleaved([side_x, side_y])
              if l + 1 < L:
                  setup_late(l + 1)

        except _Stop:
            pass
        Sd.finish("sp")
        Sd.emit(block)
    return nc


def _consts(S):
    cf = np.zeros((128, NCF), np.float64)
    idx = np.arange(128)
    cf[:, K_ID:K_ID + 128] = np.eye(128)
    cf[:, K_TRI:K_TRI + 128] = (idx[:, None] <= idx[None, :])
    cf[:, K_SEL:K_SEL + 128] = (idx[:, None] == 127)
    sm = ((idx[:, None] // 64) <= (idx[None, :] // 64)).astype(np.float32)
    gam = 1.0 - np.exp2(-5.0 - np.arange(6))
    lg = np.log(gam)
    for p in range(3):
        for s in range(2):
            h = 2 * p + s
            cf[s * 64:(s + 1) * 64, K_QD + p * 128:K_QD + (p + 1) * 128] = np.exp(lg[h] * (idx[None, :] + 1.0)) * 0.125
            cf[s * 64:(s + 1) * 64, K_CD + p * 64:K_CD + (p + 1) * 64] = np.exp(lg[h] * 128.0)
    for h in range(6):
        cf[:, K_KD + h] = np.exp(lg[h] * (127.0 - idx))
    cb = np.zeros((128, NCB), np.float64)
    cb[:, KB_ID:KB_ID + 128] = np.eye(128)
    cb[:, KB_NEG:KB_NEG + 128] = np.where(idx[:, None] <= idx[None, :], 0.0, -30000.0)
    j = idx[:, None]
    i = idx[None, :]
    same = (j // 64) == (i // 64)
    fwd = (j // 64 == 0) & (i // 64 == 1)
    for h in range(6):
        m = np.where(same, np.exp(lg[h] * np.abs(i - j)), np.where(fwd, np.exp(lg[h] * (i - j)), 0.0)) * 0.125
        cb[:, KB_RM + h * 128:KB_RM + (h + 1) * 128] = m
    half = 32
    inv = 10000.0 ** (-np.arange(half, dtype=np.float64) / half)
    ang = np.arange(S, dtype=np.float64)[:, None] * inv[None, :]
    cs = np.concatenate([np.cos(ang), np.cos(ang), np.sin(ang), -np.sin(ang)], axis=1)
    return cf.astype(np.float32), cb.astype(ml_dtypes.bfloat16), cs.astype(np.float32), sm


_CACHE = {}


def run(x, pre_gain, post_gain, w_in, b_forget, a_norm_gain, a_spatial_w, a_spatial_b, w_out):
    x = np.asarray(x, np.float32)
    Bn, S, _ = x.shape
    L = w_in.shape[0]
    key = (S, L)
    if key not in _CACHE:
        _CACHE[key] = build_program(S, L)
    nc = _CACHE[key]
    cf, cb, cs, sm = _consts(S)
    f = lambda a: np.ascontiguousarray(np.asarray(a, np.float32))
    shared = {
        "w_in": f(w_in), "w_out": f(w_out),
        "pgT": f(np.asarray(pre_gain).reshape(L, 8, 128).transpose(0, 2, 1)),
        "postg": f(np.broadcast_to(np.asarray(post_gain)[:, None, :], (L, 128, D))),
        "ang": f(np.broadcast_to(np.asarray(a_norm_gain).reshape(L, 1, 256), (L, 128, 256))),
        "asw": f(a_spatial_w),
        "asb": f(np.asarray(a_spatial_b).transpose(0, 2, 1)),
        "bfb": f(np.broadcast_to(np.asarray(b_forget)[:, None, :], (L, 128, 6))),
        "cstf": cf, "cstb": cb, "cs": cs, "smask": sm,
    }
    in_maps = []
    for b in range(Bn):
        m = dict(shared)
        m["x"] = np.ascontiguousarray(x[b])
        in_maps.append(m)
    res = run_bass_kernel_spmd(nc, in_maps, core_ids=list(range(Bn)))
    return np.stack([np.asarray(r["out"], np.float32) for r in res.results], axis=0)


def kernel(x, pre_gain, post_gain, w_in, b_forget, a_norm_gain, a_spatial_w, a_spatial_b, w_out):
    return run(x, pre_gain, post_gain, w_in, b_forget, a_norm_gain, a_spatial_w, a_spatial_b, w_out)
```

```python
import math
from contextlib import ExitStack

import numpy as np
import ml_dtypes
import concourse.bass as bass
import concourse.mybir as mybir
from concourse.bass_utils import run_bass_kernel_spmd

F32 = mybir.dt.float32
BF16 = mybir.dt.bfloat16
AF = mybir.ActivationFunctionType
ALU = mybir.AluOpType
AX = mybir.AxisListType

D = 1024
DIN = 3846
EPS = 1e-6
ENGS = ("pe", "act", "dve", "pool", "sp")

C_AU, C_AV, C_AG = 0, 256, 512
C_RQ, C_RK, C_RV, C_RG = 768, 1152, 1536, 1920
C_FQ, C_FK, C_FV, C_FG, C_FL = 2304, 2688, 3072, 3456, 3840

K_ID, K_TRI, K_SEL, K_QD, K_KD, K_CD = 0, 128, 256, 384, 768, 774
NCF = 774 + 192
KB_ID, KB_NEG, KB_RM = 0, 128, 256
NCB = 256 + 768


class Buf:
    __slots__ = ("name", "lw", "rd", "dsem", "excl")

    def __init__(self, name):
        self.name = name
        self.excl = name in ("pin0", "pin1", "ptr", "pmx", "pss0", "pss1", "pso0", "pso1")
        self.lw = None
        self.rd = {}
        self.dsem = None


class Sched:
    def __init__(self, nc, es):
        self.nc = nc
        self.es = es
        self.prog = {e: [] for e in ENGS}
        self.sems = {}
        self.cnt = {}
        self.seen = {e: {} for e in ENGS}
        self.capture = None
        for e in ENGS:
            self.sems[e] = es.enter_context(nc.semaphore("sem_" + e))
            self.cnt[e] = 0

    def _dsem(self, buf):
        if buf.dsem is None:
            key = "dsem_" + buf.name
            self.sems[key] = self.es.enter_context(self.nc.semaphore(key))
            self.cnt[key] = 0
            buf.dsem = key
        return buf.dsem

    def _wait(self, eng, ev):
        if ev is None:
            return
        key, val = ev
        if self.seen[eng].get(key, 0) >= val:
            return
        self.seen[eng][key] = val
        sem = self.sems[key]
        self.prog[eng].append(lambda E: E.wait_ge(sem, val))

    DEF_COST = {"pe": 0.3, "act": 0.6, "dve": 0.45, "pool": 0.8, "sp": 3.0}

    def op(self, eng, fns, reads=(), writes=(), owner=None, cost=None):
        if cost is None:
            cost = self.DEF_COST[eng]
        if self.capture is not None:
            self.capture.append((eng, fns, list(reads), list(writes), owner, cost))
            return
        if not isinstance(fns, (list, tuple)):
            fns = [fns]
        writes = list(writes) + [b for b in reads if b.excl and b not in writes]
        reads = [b for b in reads if not b.excl]
        for b in reads:
            self._wait(eng, b.lw)
        for b in writes:
            ev = b.lw
            if ev is not None and not (ev[0] == eng and eng == "pe"):
                self._wait(eng, ev)
            for k, v in b.rd.items():
                if k == eng and eng == "pe":
                    continue
                self._wait(eng, (k, v))
        if owner is None:
            key, inc = eng, 1
        else:
            key, inc = self._dsem(owner), 16
        self.cnt[key] += inc
        val = self.cnt[key]
        sem = self.sems[key]
        for f in fns[:-1]:
            self.prog[eng].append(f)
        last = fns[-1]
        self.prog[eng].append(lambda E: last(E).then_inc(sem, inc))
        ev = (key, val)
        for b in reads:
            if b.rd.get(key, 0) < val:
                b.rd[key] = val
        for b in writes:
            b.lw = ev
            b.rd = {}
        return ev

    def run_interleaved(self, chains):
        outer = self.capture
        lists = []
        for fn in chains:
            self.capture = []
            fn()
            lists.append(self.capture)
        self.capture = outer
        lists = [x for x in lists if x]
        idx = [0] * len(lists)
        left = sum(len(x) for x in lists)
        eng_free = {e: 0.0 for e in ENGS}
        tw, trd = {}, {}
        SYNC = 0.15

        def start_time(o):
            eng, fns, reads, writes, owner, cost = o
            t = eng_free[eng]
            for b_ in reads:
                if b_.excl:
                    t = max(t, tw.get(b_, 0.0) + SYNC, trd.get(b_, 0.0) + SYNC)
                else:
                    t = max(t, tw.get(b_, 0.0) + SYNC)
            for b_ in writes:
                t = max(t, tw.get(b_, 0.0) + SYNC, trd.get(b_, 0.0) + SYNC)
            return t

        while left:
            best, bt = None, None
            for c in range(len(lists)):
                if idx[c] < len(lists[c]):
                    t = start_time(lists[c][idx[c]])
                    rem = len(lists[c]) - idx[c]
                    key = (t, -rem)
                    if bt is None or key < bt:
                        best, bt = c, key
            o = lists[best][idx[best]]
            eng, fns, reads, writes, owner, cost = o
            t0 = bt[0]
            t1 = t0 + cost
            eng_free[eng] = t1 if owner is None else t0 + 0.1
            for b_ in reads:
                if b_.excl:
                    tw[b_] = max(tw.get(b_, 0.0), t1)
                else:
                    trd[b_] = max(trd.get(b_, 0.0), t1)
            for b_ in writes:
                tw[b_] = t1
                trd[b_] = 0.0
            self.op(*o)
            idx[best] += 1
            left -= 1

    def finish(self, eng="sp"):
        for key, val in self.cnt.items():
            if val > 0:
                self._wait(eng, (key, val))

    def emit(self, block):
        prog = self.prog

        @block.tensor
        def _(E):
            for f in prog["pe"]:
                f(E)

        @block.scalar
        def _(E):
            for f in prog["act"]:
                f(E)

        @block.vector
        def _(E):
            for f in prog["dve"]:
                f(E)

        @block.gpsimd
        def _(E):
            for f in prog["pool"]:
                f(E)

        @block.sync
        def _(E):
            for f in prog["sp"]:
                f(E)


class _Stop(Exception):
    pass


def build_program(S, L):
    import os
    KSTOP = int(os.environ.get("KSTOP", "0"))

    def stage(n):
        if KSTOP and n >= KSTOP:
            raise _Stop()

    NT = S // 128
    NS = S // 512
    nc = bass.Bass("TRN2", target_bir_lowering=False)

    def din(name, shape, dt=F32):
        return nc.dram_tensor(name, list(shape), dt, kind="ExternalInput").ap()

    x_d = din("x", [S, D])
    win_d = din("w_in", [L, D, DIN])
    wout_d = din("w_out", [L, D, D])
    pgT_d = din("pgT", [L, 128, 8])
    postg_d = din("postg", [L, 128, D])
    ang_d = din("ang", [L, 128, 256])
    asw_d = din("asw", [L, 4, 128, 128])
    asb_d = din("asb", [L, 128, 4])
    bfb_d = din("bfb", [L, 128, 6])
    cstf_d = din("cstf", [128, NCF])
    cstb_d = din("cstb", [128, NCB], BF16)
    cs_d = din("cs", [S, 128])
    smask_d = din("smask", [128, 128])
    out_d = nc.dram_tensor("out", [S, D], F32, kind="ExternalOutput").ap()
    xs_d = nc.dram_tensor("xscr", [S, D], F32, kind="Internal").ap() if L > 1 else None

    with ExitStack() as es:
        bufs = {}

        def B(name):
            if name not in bufs:
                bufs[name] = Buf(name)
            return bufs[name]

        def sb(name, shape, dt):
            return es.enter_context(nc.sbuf_tensor("s_" + name, list(shape), dt))

        def ps(name, shape, dt=F32):
            return es.enter_context(nc.psum_tensor("p_" + name, list(shape), dt))

        Wb = sb("Wb", [128, 8, DIN], BF16)
        Wo = sb("Wo", [128, 8, D], BF16)
        kT2 = sb("kT2", [128, 3, S], BF16)
        V = sb("V", [128, NT, 6, 66], BF16)
        NXA = 2
        xa = [sb(f"xa{i}", [128, D], F32) for i in range(NXA)]
        hT2 = [sb(f"hT{i}", [128, 8, 128], BF16) for i in range(2)]
        y2 = [sb(f"y{i}", [128, 4, D], BF16) for i in range(2)]
        xn = sb("xn", [128, D], BF16)
        cstf = sb("cstf", [128, NCF], F32)
        cstb = sb("cstb", [128, NCB], BF16)
        postg = sb("postg", [128, D], F32)
        ang = sb("ang", [128, 256], F32)
        asb = sb("asb", [128, 4], F32)
        bfb = sb("bfb", [128, 6], F32)
        pgT = sb("pgT", [128, 8], F32)
        WsT = sb("WsT", [128, 4, 128], BF16)
        cst = [sb(f"cs{i}", [128, 128], F32) for i in range(2)]
        sc = [sb(f"sc{i}", [128, 512], F32) for i in range(4)]
        rsA = sb("rsA", [128, 384], F32)
        rsB = sb("rsB", [128, 384], F32)
        rsE = sb("rsE", [128, 384], F32)
        vln = sb("vln", [128, 256], BF16)
        qr = sb("qr", [128, 384], BF16)
        kr = sb("kr", [128, 384], BF16)
        kd = sb("kd", [128, 384], BF16)
        qdT = sb("qdT", [128, 3, 128], BF16)
        kT = sb("kT", [128, 3, 128], BF16)
        vr = sb("vr", [128, 384], BF16)
        PsT = sb("PsT", [128, 6, 128], BF16)
        Sst = sb("Sst", [128, 3, 64], F32)
        Sbf = sb("Sbf", [128, 3, 64], BF16)
        sgf2 = [sb(f"sgf{i}", [128, 4, 384], BF16) for i in range(2)]
        ncum = sb("ncum", [128, NT, 6], F32)
        nbI = sb("nbI", [128, NT, 6], F32)
        qT2b = [sb(f"qT2{i}", [128, 3, 512], BF16) for i in range(2)]
        NPT = 2
        PT = [sb(f"PT{i}", [128, 512], BF16) for i in range(NPT)]
        st = sb("st", [128, 64], F32)
        qz = sb("qz", [128, 2, 512], BF16)
        qTz = sb("qTz", [128, 6, 128], BF16)

        pin = [ps(f"pin{i}", [128, 512]) for i in range(2)]
        ptr = ps("ptr", [128, 8, 128], BF16)
        pmx = ps("pmx", [128, 512])
        pss = [ps(f"pss{i}", [128, 512]) for i in range(2)]
        pso = [ps(f"pso{i}", [128, 4, 128]) for i in range(2)]

        Sd = Sched(nc, es)
        block = es.enter_context(nc.Block())
        pso1_b = pso[1][:].rearrange("p a b -> p (a b)").bitcast(BF16).rearrange("p (a b) -> p a b", a=8)
        ptr_f = ptr[:].rearrange("p a b -> p (a b)").bitcast(F32)
        xn_f = xn[:].bitcast(F32)
        pss0_b = pss[0][:].bitcast(BF16).rearrange("p (a b) -> p a b", a=8)
        pso0_f = pso[0][:].rearrange("p a b -> p (a b)")
        PT0v = PT[0][:].rearrange("p (a b) -> p a b", a=4)
        PT1v = PT[1][:].rearrange("p (a b) -> p a b", a=4)
        op = Sd.op

        ident_f = cstf[:, K_ID:K_ID + 128]
        tri = cstf[:, K_TRI:K_TRI + 128]
        sel = cstf[:, K_SEL:K_SEL + 128]
        qdec = cstf[:, K_QD:K_QD + 384].rearrange("p (a b) -> p a b", a=3)
        kdec = cstf[:, K_KD:K_KD + 6]
        cd2t = cstf[:, K_CD:K_CD + 192].rearrange("p (a b) -> p a b", a=3)
        ident_b = cstb[:, KB_ID:KB_ID + 128]
        negm = cstb[:, KB_NEG:KB_NEG + 128]
        rmask = cstb[:, KB_RM:KB_RM + 768].rearrange("p (a b) -> p a b", a=6)

        def fsz(ap):
            n = 1
            for d in ap.shape[1:]:
                n *= d
            return n

        def dma(out, in_, reads, writes, owner, eng="sp"):
            op(eng, lambda E: E.dma_start(out=out, in_=in_), reads, writes, owner=owner, cost=3.0)

        def act(out, in_, func, reads, writes, bias=0.0, scale=1.0, accum=None):
            c = 0.25 + fsz(out) / 1200.0 + (0.1 if accum is not None else 0.0)
            if accum is None:
                op("act", lambda E: E.activation(out=out, in_=in_, func=func, bias=bias, scale=scale), reads, writes, cost=c)
            else:
                op("act", lambda E: E.activation(out=out, in_=in_, func=func, bias=bias, scale=scale, accum_out=accum), reads, writes, cost=c)

        def vcost(eng, out, two_src):
            n = fsz(out)
            if eng == "pool":
                return 0.35 + n / 700.0
            return 0.16 + n / (960.0 if two_src else 1600.0)

        def tt(eng, out, in0, in1, alu, reads, writes):
            op(eng, lambda E: E.tensor_tensor(out=out, in0=in0, in1=in1, op=alu), reads, writes, cost=vcost(eng, out, True))

        def ts(eng, out, in0, s1, s2, op0, op1, reads, writes):
            c = vcost(eng, out, False)
            if op1 is None:
                op(eng, lambda E: E.tensor_scalar(out=out, in0=in0, scalar1=s1, scalar2=None, op0=op0), reads, writes, cost=c)
            else:
                op(eng, lambda E: E.tensor_scalar(out=out, in0=in0, scalar1=s1, scalar2=s2, op0=op0, op1=op1), reads, writes, cost=c)

        def stt(eng, out, in0, scalar, in1, op0, op1, reads, writes):
            op(eng, lambda E: E.scalar_tensor_tensor(out=out, in0=in0, scalar=scalar, in1=in1, op0=op0, op1=op1), reads, writes,
               cost=vcost(eng, out, True))

        def cp(eng, out, in_, reads, writes):
            if eng == "act":
                op("act", lambda E: E.copy(out=out, in_=in_), reads, writes, cost=0.25 + fsz(out) / 1200.0)
            else:
                op(eng, lambda E: E.tensor_copy(out=out, in_=in_), reads, writes, cost=vcost(eng, out, False))

        def mm(specs, reads, writes):
            fns = []
            c = 0.0
            for sp_ in specs:
                (o, l, r, s0, s1) = sp_[:5]
                sk = len(sp_) > 5
                c += 0.04 + max(64, fsz(o)) / 1600.0
                fns.append(lambda E, o=o, l=l, r=r, s0=s0, s1=s1, sk=sk: E.matmul(o, lhsT=l, rhs=r, start=s0, stop=s1, skip_group_check=sk))
            op("pe", fns, reads, writes, cost=c)

        def tr(specs, reads, writes):
            fns = []
            for (o, i, idn) in specs:
                fns.append(lambda E, o=o, i=i, idn=idn: E.transpose(out=o, in_=i, identity=idn))
            op("pe", fns, reads, writes, cost=0.1 * len(specs))

        def silu_chain(src_ps, src_buf, tmp, tmp_buf, out, out_buf, n_extra_reads=()):
            act(tmp, src_ps, AF.Exp, [src_buf], [tmp_buf], scale=-1.0)
            act(tmp, tmp, AF.Ln, [tmp_buf], [tmp_buf], bias=1.0)
            act(tmp, tmp, AF.Exp, [tmp_buf], [tmp_buf], scale=-1.0)
            tt("dve", out, src_ps, tmp, ALU.mult, [src_buf, tmp_buf], [out_buf])

        def rstd_small(var_ap, out_ap, buf, scale=1.0):
            act(out_ap, var_ap, AF.Ln, [buf], [buf], bias=EPS, scale=scale)
            act(out_ap, out_ap, AF.Exp, [buf], [buf], scale=-0.5)

        dma(cstf[:], cstf_d[:, :], [], [B("cstf")], B("cstf"))
        dma(cstb[:], cstb_d[:, :], [], [B("cstb")], B("cstb"))
        op("pool", lambda E: E.memset(V[:, :, :, 64:66], 1.0), [], [B("Vones")])
        op("pool", lambda E: E.memset(qz[:], 0.0), [], [B("qz")])
        op("pool", lambda E: E.memset(qTz[:], 0.0), [], [B("qTz")])

        xa_i = [0]

        xa_mode = ["both"]

        def next_xa():
            if xa_mode[0] == "y":
                return xa[0], B("xa0")
            i = xa_i[0] % NXA
            xa_i[0] += 1
            return xa[i], B(f"xa{i}")


        try:
          stage(1)
          for l in range(L):
              src_d = x_d if l == 0 else xs_d
              dst_d = out_d if l == L - 1 else xs_d
              srcB = B("xsrc") if l == 0 else B("xscr")
              dstB = B("out") if l == L - 1 else B("xscr")

              def setup_early(l):
                  dma(pgT[:], pgT_d[l], [], [B("pgT")], B("pgT"))
                  dma(ang[:], ang_d[l], [], [B("ang")], B("ang"))
                  dma(asb[:], asb_d[l], [], [B("asb")], B("asb"))
                  dma(bfb[:], bfb_d[l], [], [B("bfb")], B("bfb"))
                  ci = 0
                  for kc in range(8):
                      for c0 in range(0, DIN, 1024):
                          c1 = min(DIN, c0 + 1024)
                          stg, stgB = next_xa()
                          dma(stg[:, 0:c1 - c0], win_d[l, kc * 128:(kc + 1) * 128, c0:c1], [], [stgB], stgB)
                          if ci % 2 == 0:
                              ts("dve", Wb[:, kc, c0:c1], stg[:, 0:c1 - c0], pgT[:, kc:kc + 1], None, ALU.mult, None,
                                 [stgB, B("pgT")], [B("Wb")])
                          else:
                              act(Wb[:, kc, c0:c1], stg[:, 0:c1 - c0], AF.Copy, [stgB, B("pgT")], [B("Wb")], scale=pgT[:, kc:kc + 1])
                          ci += 1
                  stg, stgB = next_xa()
                  stg4 = stg[:, 0:512].rearrange("p (g j) -> p g j", g=4)
                  dma(stg4, asw_d[l].rearrange("g i j -> i g j"), [], [stgB], stgB)
                  pm4 = pmx[:, 0:512].rearrange("p (g i) -> p g i", g=4)
                  tr([(pm4[:, g, :], stg4[:, g, :], ident_f) for g in range(4)], [stgB, B("cstf")], [B("pmx")])
                  stg2, stg2B = next_xa()
                  dma(stg2[:, 0:128], smask_d[:, :], [], [stg2B], stg2B)
                  tt("dve", WsT[:], pm4, stg2[:, 0:128].unsqueeze(1).broadcast_to([128, 4, 128]), ALU.mult,
                     [B("pmx"), stg2B], [B("WsT")])

              def setup_late(l):
                  dma(postg[:], postg_d[l], [], [B("postg")], B("postg"))
                  for kc in range(8):
                      stg, stgB = next_xa()
                      dma(stg[:], wout_d[l, kc * 128:(kc + 1) * 128, :], [], [stgB], stgB)
                      cp(["dve", "act"][kc % 2], Wo[:, kc, :], stg[:], [stgB], [B("Wo")])

              if l == 0:
                  setup_early(0)
                  setup_late(0)
              stage(2)
              stage(3)
              op("pool", lambda E: E.memset(Sst[:], 0.0), [], [B("Sst")])
              op("pool", lambda E: E.memset(Sbf[:], 0.0), [], [B("Sbf")])

              def pre(t):
                  tl = t % 4
                  xt, xtB = xa[0], B("xa0")
                  dma(xt[:], src_d[t * 128:(t + 1) * 128, :], [srcB], [xtB], xtB)
                  cs_t, csB = cst[t % 2], B(f"cs{t % 2}")
                  dma(cs_t[:], cs_d[t * 128:(t + 1) * 128, :], [], [csB], csB)
                  act(xn[:], xt[:], AF.Square, [xtB], [B("xn"), B("st_ss")], accum=st[:, 0:1])
                  rstd_small(st[:, 0:1], st[:, 1:2], B("st_ss"), scale=1.0 / D)
                  ts("dve", xn[:], xt[:], st[:, 1:2], None, ALU.mult, None, [xtB, B("st_ss")], [B("xn")])
                  tr([(ptr[:, kc, :], xn[:, kc * 128:(kc + 1) * 128], ident_b) for kc in range(8)],
                     [B("xn"), B("cstb")], [B("ptr")])
                  hB = B(f"hT{t % 2}")
                  cp("dve", hT2[t % 2][:], ptr[:], [B("ptr")], [hB])


              def chA(t):
                  tl = t % 4
                  hB = B(f"hT{t % 2}")
                  Ib = (t // 4) % 2
                  cs_t, csB = cst[t % 2], B(f"cs{t % 2}")
                  pin_i = [0]

                  def inproj(c0, n):
                      k = 0
                      pin_i[0] += 1
                      pb, pbB = pin[k], B(f"pin{k}")
                      mm([(pb[:, 0:n], hT2[t % 2][:, kc, :], Wb[:, kc, c0:c0 + n], kc == 0, kc == 7)
                          for kc in range(8)], [hB, B("Wb")], [pbB])
                      return pb, pbB

                  pa, paB = inproj(C_AU, 512)
                  cA, cAB = sc[0], B("sc0")
                  eA, eAB = sc[1], B("sc1")
                  guv, guvB = sc[2], B("sc2")
                  act(cA[:], pa[:], AF.Square, [paB], [cAB])
                  ts("pool", cA[:], cA[:], 0.044715, 1.0, ALU.mult, ALU.add, [cAB], [cAB])
                  tt("dve", cA[:], cA[:], pa[:], ALU.mult, [cAB, paB], [cAB])
                  act(eA[:], cA[:], AF.Exp, [cAB], [eAB], scale=-2.0 * math.sqrt(2.0 / math.pi))
                  act(eA[:], eA[:], AF.Ln, [eAB], [eAB], bias=1.0)
                  act(eA[:], eA[:], AF.Exp, [eAB], [eAB], scale=-1.0)
                  tt("dve", guv[:], pa[:], eA[:], ALU.mult, [paB, eAB], [guvB])
                  gv = guv[:, 256:512]
                  gv3 = gv.rearrange("p (g c) -> p g c", g=4)
                  sq3 = cA[:, 0:256].rearrange("p (g c) -> p g c", g=4)
                  stA = B("st_A")
                  op("dve", lambda E: E.tensor_reduce(out=st[:, 8:12], in_=gv3, axis=AX.X, op=ALU.add), [guvB], [stA])
                  tt("pool", cA[:, 0:256], gv, gv, ALU.mult, [guvB], [cAB])
                  op("dve", lambda E: E.tensor_reduce(out=st[:, 12:16], in_=sq3, axis=AX.X, op=ALU.add), [cAB], [stA])
                  ts("dve", st[:, 8:12], st[:, 8:12], 1.0 / 64, None, ALU.mult, None, [stA], [stA])
                  tt("dve", st[:, 16:20], st[:, 8:12], st[:, 8:12], ALU.mult, [stA], [stA])
                  stt("dve", st[:, 12:16], st[:, 12:16], 1.0 / 64, st[:, 16:20], ALU.mult, ALU.subtract, [stA], [stA])
                  rstd_small(st[:, 12:16], st[:, 12:16], stA)
                  vc, vcB = sc[3][:, 0:256], B("sc3a")
                  vc3 = vc.rearrange("p (g c) -> p g c", g=4)
                  tt("dve", vc3, gv3, st[:, 8:12].unsqueeze(2).broadcast_to([128, 4, 64]), ALU.subtract, [guvB, stA], [vcB])
                  tt("dve", vc3, vc3, st[:, 12:16].unsqueeze(2).broadcast_to([128, 4, 64]), ALU.mult, [vcB, stA], [vcB])
                  tt("pool", vln[:], vc, ang[:], ALU.mult, [vcB, B("ang")], [B("vln")])
                  pm = pmx[:, 0:256]
                  pm3 = pm.rearrange("p (g c) -> p g c", g=4)
                  mm([(pmx[:, g * 64:(g + 1) * 64], WsT[:, g, :], vln[:, g * 64:(g + 1) * 64], True, True) for g in range(4)],
                     [B("WsT"), B("vln")], [B("pmx")])
                  mb, mbB = sc[3][:, 256:512], B("sc3b")
                  mb3 = mb.rearrange("p (g c) -> p g c", g=4)
                  tt("dve", mb3, pm3, asb[:, 0:4].unsqueeze(2).broadcast_to([128, 4, 64]), ALU.add, [B("pmx"), B("asb")], [mbB])
                  tt("pool", mb, mb, guv[:, 0:256], ALU.mult, [mbB, guvB], [mbB])
                  pg_, pgB = inproj(C_AG, 256)
                  e2, e2B = sc[1][:, 0:256], B("sc1")
                  silu_chain(pg_[:, 0:256], pgB, e2, e2B, e2, e2B)
                  tt("pool", y2[Ib][:, tl, 0:256], mb, e2, ALU.mult, [mbB, e2B], [B(f"yA{Ib}_{tl}")])


              def chR(t):
                  tl = t % 4
                  hB = B(f"hT{t % 2}")
                  Ib = (t // 4) % 2
                  cs_t, csB = cst[t % 2], B(f"cs{t % 2}")
                  pin_i = [0]

                  def inproj(c0, n):
                      k = 1
                      pin_i[0] += 1
                      pb, pbB = pin[k], B(f"pin{k}")
                      mm([(pb[:, 0:n], hT2[t % 2][:, kc, :], Wb[:, kc, c0:c0 + n], kc == 0, kc == 7)
                          for kc in range(8)], [hB, B("Wb")], [pbB])
                      return pb, pbB

                  cos2 = cs_t[:, 0:64].unsqueeze(1).broadcast_to([128, 6, 64])
                  sin_b = cs_t[:, 64:96].unsqueeze(1).broadcast_to([128, 6, 32])
                  nsin_b = cs_t[:, 96:128].unsqueeze(1).broadcast_to([128, 6, 32])
                  tA, tAB = rsA, B("rsA")
                  tB_, tBB = rsB, B("rsB")
                  for (c0, dst, dstB) in ((C_RQ, qr, B("qr")), (C_RK, kr, B("kr"))):
                      pq, pqB = inproj(c0, 384)
                      pq3 = pq[:, 0:384].rearrange("p (h c) -> p h c", h=6)
                      pq4 = pq[:, 0:384].rearrange("p (h s f) -> p h s f", h=6, s=2)
                      tA3 = tA[:, 0:384].rearrange("p (h c) -> p h c", h=6)
                      tB4 = tB_[:, 0:384].rearrange("p (h s f) -> p h s f", h=6, s=2)
                      tt("dve", tA3, pq3, cos2, ALU.mult, [pqB, csB], [tAB])
                      tt("dve", tB4[:, :, 0, :], pq4[:, :, 1, :], nsin_b, ALU.mult, [pqB, csB], [tBB])
                      tt("dve", tB4[:, :, 1, :], pq4[:, :, 0, :], sin_b, ALU.mult, [pqB, csB], [tBB])
                      tt("dve", dst[:], tA[:, 0:384], tB_[:, 0:384], ALU.add, [tAB, tBB], [dstB])
                  tt("pool", kd[:].rearrange("p (h c) -> p h c", h=6), kr[:].rearrange("p (h c) -> p h c", h=6),
                     kdec.unsqueeze(2).broadcast_to([128, 6, 64]), ALU.mult, [B("kr"), B("cstf")], [B("kd")])
                  tr([(pso1_b[:, p, :], qr[:, p * 128:(p + 1) * 128], ident_b) for p in range(3)]
                     + [(pso1_b[:, 3 + p, :], kr[:, p * 128:(p + 1) * 128], ident_b) for p in range(3)],
                     [B("qr"), B("kr"), B("cstb")], [B("pso1")])
                  cp("dve", qTz[0:64, 0:6:2, :], pso1_b[0:64, 0:3, :], [B("pso1")], [B("qTz")])
                  cp("dve", qTz[64:128, 1:6:2, :], pso1_b[64:128, 0:3, :], [B("pso1")], [B("qTz")])
                  tt("dve", qdT[:], pso1_b[:, 0:3, :], qdec, ALU.mult, [B("pso1"), B("cstf")], [B("qdT")])
                  cp("act", kT[:], pso1_b[:, 3:6, :], [B("pso1")], [B("kT")])
                  pv, pvB = inproj(C_RV, 384)
                  cp("act", vr[:], pv[:, 0:384], [pvB], [B("vr")])
                  pgr, pgrB = inproj(C_RG, 384)
                  eR, eRB = rsE[:, 0:384], B("rsE")
                  silu_chain(pgr[:, 0:384], pgrB, eR, eRB, eR, eRB, ())
                  specs0, specs1 = [], []
                  for h in range(6):
                      p_, s_ = h // 2, h % 2
                      rows = slice(s_ * 64, (s_ + 1) * 64)
                      if s_ == 0:
                          specs0.append((pso[1][:].rearrange("p a b -> p (a b)")[:, p_ * 128:(p_ + 1) * 128], kT[:, p_, :], qTz[:, h, :], True, True))
                      else:
                          specs1.append((pin[1][:, p_ * 128:(p_ + 1) * 128], kT[:, p_, :], qTz[:, h, :], True, True))
                  mm(specs0, [B("kT"), B("qTz")], [B("pso1")])
                  mm(specs1, [B("kT"), B("qTz")], [B("pin1")])
                  tt("dve", PsT[:, 0:6:2, :], pso[1][:, 0:3, :], rmask[:, 0:6:2, :], ALU.mult,
                     [B("pso1"), B("cstb")], [B("PsT0")])
                  tt("dve", PsT[:, 1:6:2, :], pin[1][:, 0:384].rearrange("p (h i) -> p h i", h=3), rmask[:, 1:6:2, :], ALU.mult,
                     [B("pin1"), B("cstb")], [B("PsT1")])
                  pro = pso[1][:].rearrange("p a b -> p (a b)")
                  specs = []
                  for h in range(6):
                      p_, s_ = h // 2, h % 2
                      rows = slice(s_ * 64, (s_ + 1) * 64)
                      specs.append((pro[:, h * 64:(h + 1) * 64], PsT[:, h, :], vr[:, h * 64:(h + 1) * 64], True, False))
                      specs.append((pro[:, h * 64:(h + 1) * 64], qdT[rows, p_, :], Sbf[rows, p_, :], False, True))
                  mm(specs, [B("PsT0"), B("PsT1"), B("vr"), B("qdT"), B("Sbf")], [B("pso1")])
                  pkv = pin[1][:, 0:384].rearrange("p (a b) -> p a b", a=3)
                  mm([(pin[1][:, p_ * 128:(p_ + 1) * 128], kd[:, p_ * 128:(p_ + 1) * 128], vr[:, p_ * 128:(p_ + 1) * 128], True, True) for p_ in range(3)],
                     [B("kd"), B("vr")], [B("pin1")])
                  tt("pool", Sst[:], Sst[:], cd2t, ALU.mult, [B("Sst"), B("cstf")], [B("Sst")])
                  tt("dve", Sst[0:64], Sst[0:64], pkv[0:64, :, 0:64], ALU.add, [B("Sst"), B("pin1")], [B("Sst")])
                  tt("dve", Sst[64:128], Sst[64:128], pkv[64:128, :, 64:128], ALU.add, [B("Sst"), B("pin1")], [B("Sst")])
                  cp("pool", Sbf[:], Sst[:], [B("Sst")], [B("Sbf")])
                  ro3 = pro[:, 0:384].rearrange("p (h c) -> p h c", h=6)
                  stR = B("st_R")
                  sqR, sqRB = rsA, B("rsA")
                  oc, ocB = rsB, B("rsB")
                  oc3 = oc[:, 0:384].rearrange("p (h c) -> p h c", h=6)
                  op("dve", lambda E: E.tensor_reduce(out=st[:, 24:30], in_=ro3, axis=AX.X, op=ALU.add), [B("pso1")], [stR])
                  act(sqR[:, 0:384], pro[:, 0:384], AF.Square, [B("pso1")], [sqRB])
                  op("dve", lambda E: E.tensor_reduce(out=st[:, 30:36], in_=sqR[:, 0:384].rearrange("p (h c) -> p h c", h=6),
                                                      axis=AX.X, op=ALU.add), [sqRB], [stR])
                  ts("dve", st[:, 24:30], st[:, 24:30], 1.0 / 64, None, ALU.mult, None, [stR], [stR])
                  tt("dve", st[:, 36:42], st[:, 24:30], st[:, 24:30], ALU.mult, [stR], [stR])
                  stt("dve", st[:, 30:36], st[:, 30:36], 1.0 / 64, st[:, 36:42], ALU.mult, ALU.subtract, [stR], [stR])
                  rstd_small(st[:, 30:36], st[:, 30:36], stR)
                  tt("dve", oc3, ro3, st[:, 24:30].unsqueeze(2).broadcast_to([128, 6, 64]), ALU.subtract, [B("pso1"), stR], [ocB])
                  tt("dve", oc3, oc3, st[:, 30:36].unsqueeze(2).broadcast_to([128, 6, 64]), ALU.mult, [ocB, stR], [ocB])
                  tt("pool", y2[Ib][:, tl, 256:640], oc[:, 0:384], eR, ALU.mult, [ocB, eRB], [B(f"yR{Ib}_{tl}")])


              def chF(t):
                  tl = t % 4
                  hB = B(f"hT{t % 2}")
                  Ib = (t // 4) % 2
                  cs_t, csB = cst[t % 2], B(f"cs{t % 2}")
                  pin_i = [0]

                  def inproj(c0, n):
                      pin_i[0] += 1
                      pb, pbB = ptr_f, B("ptr")
                      mm([(pb[:, 0:n], hT2[t % 2][:, kc, :], Wb[:, kc, c0:c0 + n], kc == 0, kc == 7)
                          for kc in range(8)], [hB, B("Wb")], [pbB])
                      return pb, pbB

                  pfv, pfvB = inproj(C_FV, 384)
                  cp("act", V[:, t, :, 0:64], pfv[:, 0:384].rearrange("p (h c) -> p h c", h=6), [pfvB], [B(f"V{t}")])
                  pfg, pfgB = inproj(C_FG, 390)
                  eF, eFB = xn_f[:, 0:384], B("xn")
                  silu_chain(pfg[:, 0:384], pfgB, eF, eFB, sgf2[Ib][:, tl, :], B(f"sgf{Ib}_{tl}"))
                  stF = B("st_F")
                  tt("dve", st[:, 44:50], pfg[:, 384:390], bfb[:], ALU.add, [pfgB, B("bfb")], [stF])
                  act(st[:, 44:50], st[:, 44:50], AF.Exp, [stF], [stF], scale=-1.0)
                  act(st[:, 44:50], st[:, 44:50], AF.Ln, [stF], [stF], bias=1.0)
                  if t == 0:
                      mm([(pmx[:, 256:262], tri, st[:, 44:50], True, True)], [B("cstf"), stF], [B("pmx")])
                  else:
                      mm([(pmx[:, 256:262], tri, st[:, 44:50], True, False),
                          (pmx[:, 256:262], sel, ncum[:, t - 1, :], False, True)],
                         [B("cstf"), stF, B(f"ncum{t - 1}")], [B("pmx")])
                  cp("dve", ncum[:, t, :], pmx[:, 256:262], [B("pmx")], [B(f"ncum{t}")])
                  for (cb_, isq) in ((C_FQ, True), (C_FK, False)):
                      specs = []
                      for p_ in range(3):
                          for kc in range(8):
                              specs.append((ptr_f[:, p_ * 128:(p_ + 1) * 128], Wb[:, kc, cb_ + p_ * 128:cb_ + (p_ + 1) * 128],
                                            hT2[t % 2][:, kc, :], kc == 0, kc == 7))
                      mm(specs, [hB, B("Wb")], [B("ptr")])
                      src3 = ptr_f[:, 0:384].rearrange("p (a b) -> p a b", a=3)
                      if isq:
                          cp("act", qT2b[Ib][:, :, tl * 128:(tl + 1) * 128], src3, [B("ptr")], [B(f"qT2_{Ib}")])
                      else:
                          cp("dve", kT2[:, :, t * 128:(t + 1) * 128], src3, [B("ptr")], [B(f"kT2_{t}")])

              def FoX(I):
                  Ib = I % 2
                  nj = 4 * I + 4
                  mm([(pss[0][:, 0:6], sel, ncum[:, 4 * I + 1, :], True, True)], [B("cstf"), B(f"ncum{4 * I + 1}")], [B("pss0")])
                  tt("dve", nbI[:, 0:nj, :], ncum[:, 0:nj, :], pss[0][:, 0:6].unsqueeze(1).broadcast_to([128, nj, 6]), ALU.subtract,
                     [B(f"ncum{j}") for j in range(nj)] + [B("pss0")], [B("nbI")])
                  steps = [(h, j) for h in range(6) for j in range(nj)]

                  def fox_S(idx):
                      h, j = steps[idx]
                      p_, s_ = h // 2, h % 2
                      rows = slice(s_ * 64, (s_ + 1) * 64)
                      a = j - 4 * I
                      lo = max(a, 0) * 128
                      psb, psB = pss[idx % 2], B(f"pss{idx % 2}")
                      kslc = kT2[:, p_, j * 128:(j + 1) * 128]
                      if s_ == 0 and j == 0:
                          cp("dve", qz[0:64, 0, :], qT2b[Ib][0:64, p_, :], [B(f"qT2_{Ib}")], [B("qz")])
                          cp("dve", qz[64:128, 1, :], qT2b[Ib][64:128, p_, :], [B(f"qT2_{Ib}")], [B("qz")])
                      if a < 0:
                          specs = [(psb[:, 0:512], kslc, qz[:, s_, 0:512], True, True)]
                      else:
                          specs = [(psb[:, lo:lo + 128], ident_b, negm, True, False),
                                   (psb[:, lo:lo + 128], kslc, qz[:, s_, lo:lo + 128], False, True)]
                          if lo + 128 < 512:
                              specs.append((psb[:, lo + 128:512], kslc, qz[:, s_, lo + 128:512], True, True))
                      mm(specs, [B(f"kT2_{j}"), B("qz"), B("cstb")], [psB])

                  def fox_PV(idx):
                      h, j = steps[idx]
                      a = j - 4 * I
                      lo = max(a, 0) * 128
                      po, poB = pso[0], B("pso0")
                      psb, psB = pss[idx % 2], B(f"pss{idx % 2}")
                      ptb, ptB = PT[idx % NPT], B(f"PT{idx % NPT}")
                      act(ptb[:, lo:512], psb[:, lo:512], AF.Exp, [psB, B("nbI")], [ptB], bias=nbI[:, j, h:h + 1], scale=0.125)
                      specs = []
                      for qt in range(lo // 128, 4):
                          specs.append((po[:, qt, 0:65], ptb[:, qt * 128:(qt + 1) * 128], V[:, j, h, 0:65], j == 0 and qt == 0, j == 4 * I + qt, True))
                      mm(specs, [ptB, B(f"V{j}"), B("Vones")], [poB])
                      if j == nj - 1:
                          op("dve", lambda E, po=po: E.reciprocal(out=st[:, 52:56], in_=po[:, :, 64:65].rearrange("p a b -> p (a b)")),
                             [poB], [B("st_O")])
                          for qt in range(4):
                              stt("dve", y2[Ib][:, qt, 640 + h * 64:640 + (h + 1) * 64], po[:, qt, 0:64], st[:, 52 + qt:53 + qt],
                                  sgf2[Ib][:, qt, h * 64:(h + 1) * 64], ALU.mult, ALU.mult,
                                  [poB, B("st_O"), B(f"sgf{Ib}_{qt}")], [B(f"yF{Ib}_{h}")])

                  fox_S(0)
                  for idx in range(len(steps)):
                      if idx + 1 < len(steps):
                          fox_S(idx + 1)
                      fox_PV(idx)

              def P3y_t(t):
                  tl = t % 4
                  Ib = (t // 4) % 2
                  yreads = [B(f"yA{Ib}_{tl}"), B(f"yR{Ib}_{tl}")] + [B(f"yF{Ib}_{h}") for h in range(6)]
                  tr([(ptr[:, kc, :], y2[Ib][:, tl, kc * 128:(kc + 1) * 128], ident_b) for kc in range(8)], yreads + [B("cstb")], [B("ptr")])
                  yTt, yTB = hT2[1], B("hT1")
                  cp("act", yTt[:], ptr[:], [B("ptr")], [yTB])
                  mm([(pin[hf][:], yTt[:, kc, :], Wo[:, kc, hf * 512:(hf + 1) * 512], kc == 0, kc == 7)
                      for kc in range(8) for hf in range(2)], [yTB, B("Wo")], [B("pin0"), B("pin1")])
                  xr, xrB = next_xa()
                  dma(xr[:], src_d[t * 128:(t + 1) * 128, :], [srcB], [xrB], xrB)
                  stP = B("st_P")
                  junk, junkB = sc[3], B("sc3")
                  tmps = [(sc[1], B("sc1")), (sc[2], B("sc2"))]
                  for hf in range(2):
                      act(junk[:], pin[hf][:], AF.Square, [B(f"pin{hf}")], [junkB, stP], accum=st[:, 58 + hf:59 + hf])
                      tt("dve", tmps[hf][0][:], pin[hf][:], postg[:, hf * 512:(hf + 1) * 512], ALU.mult,
                         [B(f"pin{hf}"), B("postg")], [tmps[hf][1]])
                  tt("dve", st[:, 60:61], st[:, 58:59], st[:, 59:60], ALU.add, [stP], [stP])
                  rstd_small(st[:, 60:61], st[:, 60:61], stP, scale=1.0 / D)
                  for hf in range(2):
                      stt("dve", xr[:, hf * 512:(hf + 1) * 512], tmps[hf][0][:], st[:, 60:61], xr[:, hf * 512:(hf + 1) * 512],
                          ALU.mult, ALU.add, [tmps[hf][1], stP, xrB], [xrB])
                  dma(dst_d[t * 128:(t + 1) * 128, :], xr[:], [xrB], [dstB], xrB)

              def P3x_t(t):
                  tl = t % 4
                  Ib = (t // 4) % 2
                  ob = [(pss[1][:], B("pss1")), (pso0_f, B("pso0"))]
                  yreads = [B(f"yA{Ib}_{tl}"), B(f"yR{Ib}_{tl}")] + [B(f"yF{Ib}_{h}") for h in range(6)]
                  tr([(pss0_b[:, kc, :], y2[Ib][:, tl, kc * 128:(kc + 1) * 128], ident_b) for kc in range(8)], yreads + [B("cstb")], [B("pss0")])
                  cp("act", PT0v, pss0_b[:, 0:4, :], [B("pss0")], [B("PT0")])
                  cp("dve", PT1v, pss0_b[:, 4:8, :], [B("pss0")], [B("PT1")])
                  specs = []
                  for kc in range(8):
                      lh = (PT0v if kc < 4 else PT1v)[:, kc % 4, :]
                      for hf in range(2):
                          specs.append((ob[hf][0], lh, Wo[:, kc, hf * 512:(hf + 1) * 512], kc == 0, kc == 7))
                  mm(specs, [B("PT0"), B("PT1"), B("Wo")], [ob[0][1], ob[1][1]])
                  xr, xrB = xa[1], B("xa1")
                  dma(xr[:], src_d[t * 128:(t + 1) * 128, :], [srcB], [xrB], xrB)
                  stP = B("st_P")
                  for hf in range(2):
                      act(pss[0][:], ob[hf][0], AF.Square, [ob[hf][1]], [B("pss0"), stP], accum=st[:, 58 + hf:59 + hf])
                      tt("dve", ob[hf][0], ob[hf][0], postg[:, hf * 512:(hf + 1) * 512], ALU.mult, [ob[hf][1], B("postg")], [ob[hf][1]])
                  tt("dve", st[:, 60:61], st[:, 58:59], st[:, 59:60], ALU.add, [stP], [stP])
                  rstd_small(st[:, 60:61], st[:, 60:61], stP, scale=1.0 / D)
                  for hf in range(2):
                      stt("dve", xr[:, hf * 512:(hf + 1) * 512], ob[hf][0], st[:, 60:61], xr[:, hf * 512:(hf + 1) * 512],
                          ALU.mult, ALU.add, [ob[hf][1], stP, xrB], [xrB])
                  dma(dst_d[t * 128:(t + 1) * 128, :], xr[:], [xrB], [dstB], xrB)

              def P1(I):
                  if I == 0:
                      pre(0)
                  for tl_ in range(4):
                      t_ = 4 * I + tl_

                      def c_a(t_=t_):
                          chA(t_)

                      def c_r(t_=t_):
                          chR(t_)

                      def c_fp(t_=t_, tl_=tl_):
                          chF(t_)
                          if t_ + 1 < NT:
                              pre(t_ + 1)

                      Sd.run_interleaved([c_a, c_r, c_fp])

              def P3x(I):
                  for tl_ in range(4):
                      P3x_t(4 * I + tl_)

              def P3y(I):
                  for tl_ in range(4):
                      P3y_t(4 * I + tl_)

              XCUT = NS // 2 - 1

              stage(8)
              P1(0)
              for I in range(NS):
                  def side_x(I=I):
                      FoX(I)
                      if I <= XCUT:
                          P3x(I)

                  def side_y(I=I):
                      if I > 0 and I - 1 > XCUT:
                          P3y(I - 1)
                      if I + 1 < NS:
                          P1(I + 1)
                      elif l + 1 < L:
                          xa_mode[0] = "y"
                          setup_early(l + 1)
                          xa_mode[0] = "both"

                  Sd.run_interleaved([side_x, side_y])
              if NS - 1 > XCUT:
                  P3y(NS - 1)
              if l + 1 < L:
                  setup_late(l + 1)

        except _Stop:
            pass
        Sd.finish("sp")
        Sd.emit(block)
    return nc


def _consts(S):
    cf = np.zeros((128, NCF), np.float64)
    idx = np.arange(128)
    cf[:, K_ID:K_ID + 128] = np.eye(128)
    cf[:, K_TRI:K_TRI + 128] = (idx[:, None] <= idx[None, :])
    cf[:, K_SEL:K_SEL + 128] = (idx[:, None] == 127)
    sm = ((idx[:, None] // 64) <= (idx[None, :] // 64)).astype(np.float32)
    gam = 1.0 - np.exp2(-5.0 - np.arange(6))
    lg = np.log(gam)
    for p in range(3):
        for s in range(2):
            h = 2 * p + s
            cf[s * 64:(s + 1) * 64, K_QD + p * 128:K_QD + (p + 1) * 128] = np.exp(lg[h] * (idx[None, :] + 1.0)) * 0.125
            cf[s * 64:(s + 1) * 64, K_CD + p * 64:K_CD + (p + 1) * 64] = np.exp(lg[h] * 128.0)
    for h in range(6):
        cf[:, K_KD + h] = np.exp(lg[h] * (127.0 - idx))
    cb = np.zeros((128, NCB), np.float64)
    cb[:, KB_ID:KB_ID + 128] = np.eye(128)
    cb[:, KB_NEG:KB_NEG + 128] = np.where(idx[:, None] <= idx[None, :], 0.0, -30000.0)
    j = idx[:, None]
    i = idx[None, :]
    same = (j // 64) == (i // 64)
    fwd = (j // 64 == 0) & (i // 64 == 1)
    for h in range(6):
        m = np.where(same, np.exp(lg[h] * np.abs(i - j)), np.where(fwd, np.exp(lg[h] * (i - j)), 0.0)) * 0.125
        cb[:, KB_RM + h * 128:KB_RM + (h + 1) * 128] = m
    half = 32
    inv = 10000.0 ** (-np.arange(half, dtype=np.float64) / half)
    ang = np.arange(S, dtype=np.float64)[:, None] * inv[None, :]
    cs = np.concatenate([np.cos(ang), np.cos(ang), np.sin(ang), -np.sin(ang)], axis=1)
    return cf.astype(np.float32), cb.astype(ml_dtypes.bfloat16), cs.astype(np.float32), sm


_CACHE = {}


def run(x, pre_gain, post_gain, w_in, b_forget, a_norm_gain, a_spatial_w, a_spatial_b, w_out):
    x = np.asarray(x, np.float32)
    Bn, S, _ = x.shape
    L = w_in.shape[0]
    key = (S, L)
    if key not in _CACHE:
        _CACHE[key] = build_program(S, L)
    nc = _CACHE[key]
    cf, cb, cs, sm = _consts(S)
    f = lambda a: np.ascontiguousarray(np.asarray(a, np.float32))
    shared = {
        "w_in": f(w_in), "w_out": f(w_out),
        "pgT": f(np.asarray(pre_gain).reshape(L, 8, 128).transpose(0, 2, 1)),
        "postg": f(np.broadcast_to(np.asarray(post_gain)[:, None, :], (L, 128, D))),
        "ang": f(np.broadcast_to(np.asarray(a_norm_gain).reshape(L, 1, 256), (L, 128, 256))),
        "asw": f(a_spatial_w),
        "asb": f(np.asarray(a_spatial_b).transpose(0, 2, 1)),
        "bfb": f(np.broadcast_to(np.asarray(b_forget)[:, None, :], (L, 128, 6))),
        "cstf": cf, "cstb": cb, "cs": cs, "smask": sm,
    }
    in_maps = []
    for b in range(Bn):
        m = dict(shared)
        m["x"] = np.ascontiguousarray(x[b])
        in_maps.append(m)
    res = run_bass_kernel_spmd(nc, in_maps, core_ids=list(range(Bn)))
    return np.stack([np.asarray(r["out"], np.float32) for r in res.results], axis=0)


def kernel(x, pre_gain, post_gain, w_in, b_forget, a_norm_gain, a_spatial_w, a_spatial_b, w_out):
    return run(x, pre_gain, post_gain, w_in, b_forget, a_norm_gain, a_spatial_w, a_spatial_b, w_out)
```

```python
import math
from contextlib import ExitStack

import numpy as np
import ml_dtypes
import concourse.bass as bass
import concourse.mybir as mybir
from concourse.bass_utils import run_bass_kernel_spmd

F32 = mybir.dt.float32
BF16 = mybir.dt.bfloat16
AF = mybir.ActivationFunctionType
ALU = mybir.AluOpType
AX = mybir.AxisListType

D = 1024
DIN = 3846
EPS = 1e-6
ENGS = ("pe", "act", "dve", "pool", "sp")

C_AU, C_AV, C_AG = 0, 256, 512
C_RQ, C_RK, C_RV, C_RG = 768, 1152, 1536, 1920
C_FQ, C_FK, C_FV, C_FG, C_FL = 2304, 2688, 3072, 3456, 3840

K_ID, K_TRI, K_SEL, K_QD, K_KD, K_CD = 0, 128, 256, 384, 768, 774
NCF = 774 + 192
KB_ID, KB_NEG, KB_RM = 0, 128, 256
NCB = 256 + 768


class Buf:
    __slots__ = ("name", "lw", "rd", "dsem", "excl")

    def __init__(self, name):
        self.name = name
        self.excl = name in ("pin0", "pin1", "ptr", "pmx", "pss0", "pss1", "pso0", "pso1")
        self.lw = None
        self.rd = {}
        self.dsem = None


class Sched:
    def __init__(self, nc, es):
        self.nc = nc
        self.es = es
        self.prog = {e: [] for e in ENGS}
        self.sems = {}
        self.cnt = {}
        self.seen = {e: {} for e in ENGS}
        self.capture = None
        for e in ENGS:
            self.sems[e] = es.enter_context(nc.semaphore("sem_" + e))
            self.cnt[e] = 0

    def _dsem(self, buf):
        if buf.dsem is None:
            key = "dsem_" + buf.name
            self.sems[key] = self.es.enter_context(self.nc.semaphore(key))
            self.cnt[key] = 0
            buf.dsem = key
        return buf.dsem

    def _wait(self, eng, ev):
        if ev is None:
            return
        key, val = ev
        if self.seen[eng].get(key, 0) >= val:
            return
        self.seen[eng][key] = val
        sem = self.sems[key]
        self.prog[eng].append(lambda E: E.wait_ge(sem, val))

    DEF_COST = {"pe": 0.3, "act": 0.6, "dve": 0.45, "pool": 0.8, "sp": 3.0}

    def op(self, eng, fns, reads=(), writes=(), owner=None, cost=None):
        if cost is None:
            cost = self.DEF_COST[eng]
        if self.capture is not None:
            self.capture.append((eng, fns, list(reads), list(writes), owner, cost))
            return
        if not isinstance(fns, (list, tuple)):
            fns = [fns]
        writes = list(writes) + [b for b in reads if b.excl and b not in writes]
        reads = [b for b in reads if not b.excl]
        for b in reads:
            self._wait(eng, b.lw)
        for b in writes:
            ev = b.lw
            if ev is not None and not (ev[0] == eng and eng == "pe"):
                self._wait(eng, ev)
            for k, v in b.rd.items():
                if k == eng and eng == "pe":
                    continue
                self._wait(eng, (k, v))
        if owner is None:
            key, inc = eng, 1
        else:
            key, inc = self._dsem(owner), 16
        self.cnt[key] += inc
        val = self.cnt[key]
        sem = self.sems[key]
        for f in fns[:-1]:
            self.prog[eng].append(f)
        last = fns[-1]
        self.prog[eng].append(lambda E: last(E).then_inc(sem, inc))
        ev = (key, val)
        for b in reads:
            if b.rd.get(key, 0) < val:
                b.rd[key] = val
        for b in writes:
            b.lw = ev
            b.rd = {}
        return ev

    def run_interleaved(self, chains):
        outer = self.capture
        lists = []
        for fn in chains:
            self.capture = []
            fn()
            lists.append(self.capture)
        self.capture = outer
        lists = [x for x in lists if x]
        idx = [0] * len(lists)
        left = sum(len(x) for x in lists)
        eng_free = {e: 0.0 for e in ENGS}
        tw, trd = {}, {}
        SYNC = 0.15

        def start_time(o):
            eng, fns, reads, writes, owner, cost = o
            t = eng_free[eng]
            for b_ in reads:
                if b_.excl:
                    t = max(t, tw.get(b_, 0.0) + SYNC, trd.get(b_, 0.0) + SYNC)
                else:
                    t = max(t, tw.get(b_, 0.0) + SYNC)
            for b_ in writes:
                t = max(t, tw.get(b_, 0.0) + SYNC, trd.get(b_, 0.0) + SYNC)
            return t

        while left:
            best, bt = None, None
            for c in range(len(lists)):
                if idx[c] < len(lists[c]):
                    t = start_time(lists[c][idx[c]])
                    rem = len(lists[c]) - idx[c]
                    key = (t, -rem)
                    if bt is None or key < bt:
                        best, bt = c, key
            o = lists[best][idx[best]]
            eng, fns, reads, writes, owner, cost = o
            t0 = bt[0]
            t1 = t0 + cost
            eng_free[eng] = t1 if owner is None else t0 + 0.1
            for b_ in reads:
                if b_.excl:
                    tw[b_] = max(tw.get(b_, 0.0), t1)
                else:
                    trd[b_] = max(trd.get(b_, 0.0), t1)
            for b_ in writes:
                tw[b_] = t1
                trd[b_] = 0.0
            self.op(*o)
            idx[best] += 1
            left -= 1

    def finish(self, eng="sp"):
        for key, val in self.cnt.items():
            if val > 0:
                self._wait(eng, (key, val))

    def emit(self, block):
        prog = self.prog

        @block.tensor
        def _(E):
            for f in prog["pe"]:
                f(E)

        @block.scalar
        def _(E):
            for f in prog["act"]:
                f(E)

        @block.vector
        def _(E):
            for f in prog["dve"]:
                f(E)

        @block.gpsimd
        def _(E):
            for f in prog["pool"]:
                f(E)

        @block.sync
        def _(E):
            for f in prog["sp"]:
                f(E)


class _Stop(Exception):
    pass


def build_program(S, L):
    import os
    KSTOP = int(os.environ.get("KSTOP", "0"))

    def stage(n):
        if KSTOP and n >= KSTOP:
            raise _Stop()

    NT = S // 128
    NS = S // 512
    nc = bass.Bass("TRN2", target_bir_lowering=False)

    def din(name, shape, dt=F32):
        return nc.dram_tensor(name, list(shape), dt, kind="ExternalInput").ap()

    x_d = din("x", [S, D])
    win_d = din("w_in", [L, D, DIN])
    wout_d = din("w_out", [L, D, D])
    pgT_d = din("pgT", [L, 128, 8])
    postg_d = din("postg", [L, 128, D])
    ang_d = din("ang", [L, 128, 256])
    asw_d = din("asw", [L, 4, 128, 128])
    asb_d = din("asb", [L, 128, 4])
    bfb_d = din("bfb", [L, 128, 6])
    cstf_d = din("cstf", [128, NCF])
    cstb_d = din("cstb", [128, NCB], BF16)
    cs_d = din("cs", [S, 128])
    smask_d = din("smask", [128, 128])
    out_d = nc.dram_tensor("out", [S, D], F32, kind="ExternalOutput").ap()
    xs_d = nc.dram_tensor("xscr", [S, D], F32, kind="Internal").ap() if L > 1 else None

    with ExitStack() as es:
        bufs = {}

        def B(name):
            if name not in bufs:
                bufs[name] = Buf(name)
            return bufs[name]

        def sb(name, shape, dt):
            return es.enter_context(nc.sbuf_tensor("s_" + name, list(shape), dt))

        def ps(name, shape, dt=F32):
            return es.enter_context(nc.psum_tensor("p_" + name, list(shape), dt))

        Wb = sb("Wb", [128, 8, DIN], BF16)
        Wo = sb("Wo", [128, 8, D], BF16)
        kT2 = sb("kT2", [128, 3, S], BF16)
        V = sb("V", [128, NT, 6, 66], BF16)
        NXA = 2
        xa = [sb(f"xa{i}", [128, D], F32) for i in range(NXA)]
        hT2 = [sb(f"hT{i}", [128, 8, 128], BF16) for i in range(2)]
        y2 = [sb(f"y{i}", [128, 4, D], BF16) for i in range(2)]
        xn = sb("xn", [128, D], BF16)
        cstf = sb("cstf", [128, NCF], F32)
        cstb = sb("cstb", [128, NCB], BF16)
        postg = sb("postg", [128, D], F32)
        ang = sb("ang", [128, 256], F32)
        asb = sb("asb", [128, 4], F32)
        bfb = sb("bfb", [128, 6], F32)
        pgT = sb("pgT", [128, 8], F32)
        WsT = sb("WsT", [128, 4, 128], BF16)
        cst = [sb(f"cs{i}", [128, 128], F32) for i in range(2)]
        sc = [sb(f"sc{i}", [128, 512], F32) for i in range(4)]
        rsA = sb("rsA", [128, 384], F32)
        rsB = sb("rsB", [128, 384], F32)
        rsE = sb("rsE", [128, 384], F32)
        vln = sb("vln", [128, 256], BF16)
        qr = sb("qr", [128, 384], BF16)
        kr = sb("kr", [128, 384], BF16)
        kd = sb("kd", [128, 384], BF16)
        qdT = sb("qdT", [128, 3, 128], BF16)
        kT = sb("kT", [128, 3, 128], BF16)
        vr = sb("vr", [128, 384], BF16)
        PsT = sb("PsT", [128, 6, 128], BF16)
        Sst = sb("Sst", [128, 3, 64], F32)
        Sbf = sb("Sbf", [128, 3, 64], BF16)
        sgf2 = [sb(f"sgf{i}", [128, 4, 384], BF16) for i in range(2)]
        ncum = sb("ncum", [128, NT, 6], F32)
        nbI = sb("nbI", [128, NT, 6], F32)
        qT2b = [sb(f"qT2{i}", [128, 3, 512], BF16) for i in range(2)]
        NPT = 2
        PT = [sb(f"PT{i}", [128, 512], BF16) for i in range(NPT)]
        st = sb("st", [128, 64], F32)
        qz = sb("qz", [128, 2, 512], BF16)
        qTz = sb("qTz", [128, 6, 128], BF16)

        pin = [ps(f"pin{i}", [128, 512]) for i in range(2)]
        ptr = ps("ptr", [128, 8, 128], BF16)
        pmx = ps("pmx", [128, 512])
        pss = [ps(f"pss{i}", [128, 512]) for i in range(2)]
        pso = [ps(f"pso{i}", [128, 4, 128]) for i in range(2)]

        Sd = Sched(nc, es)
        block = es.enter_context(nc.Block())
        pso1_b = pso[1][:].rearrange("p a b -> p (a b)").bitcast(BF16).rearrange("p (a b) -> p a b", a=8)
        ptr_f = ptr[:].rearrange("p a b -> p (a b)").bitcast(F32)
        xn_f = xn[:].bitcast(F32)
        op = Sd.op

        ident_f = cstf[:, K_ID:K_ID + 128]
        tri = cstf[:, K_TRI:K_TRI + 128]
        sel = cstf[:, K_SEL:K_SEL + 128]
        qdec = cstf[:, K_QD:K_QD + 384].rearrange("p (a b) -> p a b", a=3)
        kdec = cstf[:, K_KD:K_KD + 6]
        cd2t = cstf[:, K_CD:K_CD + 192].rearrange("p (a b) -> p a b", a=3)
        ident_b = cstb[:, KB_ID:KB_ID + 128]
        negm = cstb[:, KB_NEG:KB_NEG + 128]
        rmask = cstb[:, KB_RM:KB_RM + 768].rearrange("p (a b) -> p a b", a=6)

        def fsz(ap):
            n = 1
            for d in ap.shape[1:]:
                n *= d
            return n

        def dma(out, in_, reads, writes, owner, eng="sp"):
            op(eng, lambda E: E.dma_start(out=out, in_=in_), reads, writes, owner=owner, cost=3.0)

        def act(out, in_, func, reads, writes, bias=0.0, scale=1.0, accum=None):
            c = 0.25 + fsz(out) / 1200.0 + (0.1 if accum is not None else 0.0)
            if accum is None:
                op("act", lambda E: E.activation(out=out, in_=in_, func=func, bias=bias, scale=scale), reads, writes, cost=c)
            else:
                op("act", lambda E: E.activation(out=out, in_=in_, func=func, bias=bias, scale=scale, accum_out=accum), reads, writes, cost=c)

        def vcost(eng, out, two_src):
            n = fsz(out)
            if eng == "pool":
                return 0.35 + n / 700.0
            return 0.16 + n / (960.0 if two_src else 1600.0)

        def tt(eng, out, in0, in1, alu, reads, writes):
            op(eng, lambda E: E.tensor_tensor(out=out, in0=in0, in1=in1, op=alu), reads, writes, cost=vcost(eng, out, True))

        def ts(eng, out, in0, s1, s2, op0, op1, reads, writes):
            c = vcost(eng, out, False)
            if op1 is None:
                op(eng, lambda E: E.tensor_scalar(out=out, in0=in0, scalar1=s1, scalar2=None, op0=op0), reads, writes, cost=c)
            else:
                op(eng, lambda E: E.tensor_scalar(out=out, in0=in0, scalar1=s1, scalar2=s2, op0=op0, op1=op1), reads, writes, cost=c)

        def stt(eng, out, in0, scalar, in1, op0, op1, reads, writes):
            op(eng, lambda E: E.scalar_tensor_tensor(out=out, in0=in0, scalar=scalar, in1=in1, op0=op0, op1=op1), reads, writes,
               cost=vcost(eng, out, True))

        def cp(eng, out, in_, reads, writes):
            if eng == "act":
                op("act", lambda E: E.copy(out=out, in_=in_), reads, writes, cost=0.25 + fsz(out) / 1200.0)
            else:
                op(eng, lambda E: E.tensor_copy(out=out, in_=in_), reads, writes, cost=vcost(eng, out, False))

        def mm(specs, reads, writes):
            fns = []
            c = 0.0
            for sp_ in specs:
                (o, l, r, s0, s1) = sp_[:5]
                sk = len(sp_) > 5
                c += 0.04 + max(64, fsz(o)) / 1600.0
                fns.append(lambda E, o=o, l=l, r=r, s0=s0, s1=s1, sk=sk: E.matmul(o, lhsT=l, rhs=r, start=s0, stop=s1, skip_group_check=sk))
            op("pe", fns, reads, writes, cost=c)

        def tr(specs, reads, writes):
            fns = []
            for (o, i, idn) in specs:
                fns.append(lambda E, o=o, i=i, idn=idn: E.transpose(out=o, in_=i, identity=idn))
            op("pe", fns, reads, writes, cost=0.1 * len(specs))

        def silu_chain(src_ps, src_buf, tmp, tmp_buf, out, out_buf, n_extra_reads=()):
            act(tmp, src_ps, AF.Exp, [src_buf], [tmp_buf], scale=-1.0)
            act(tmp, tmp, AF.Ln, [tmp_buf], [tmp_buf], bias=1.0)
            act(tmp, tmp, AF.Exp, [tmp_buf], [tmp_buf], scale=-1.0)
            tt("dve", out, src_ps, tmp, ALU.mult, [src_buf, tmp_buf], [out_buf])

        def rstd_small(var_ap, out_ap, buf, scale=1.0):
            act(out_ap, var_ap, AF.Ln, [buf], [buf], bias=EPS, scale=scale)
            act(out_ap, out_ap, AF.Exp, [buf], [buf], scale=-0.5)

        dma(cstf[:], cstf_d[:, :], [], [B("cstf")], B("cstf"))
        dma(cstb[:], cstb_d[:, :], [], [B("cstb")], B("cstb"))
        op("pool", lambda E: E.memset(V[:, :, :, 64:66], 1.0), [], [B("Vones")])
        op("pool", lambda E: E.memset(qz[:], 0.0), [], [B("qz")])
        op("pool", lambda E: E.memset(qTz[:], 0.0), [], [B("qTz")])

        xa_i = [0]

        def next_xa():
            i = xa_i[0] % NXA
            xa_i[0] += 1
            return xa[i], B(f"xa{i}")

        try:
          stage(1)
          for l in range(L):
              src_d = x_d if l == 0 else xs_d
              dst_d = out_d if l == L - 1 else xs_d
              srcB = B("xsrc") if l == 0 else B("xscr")
              dstB = B("out") if l == L - 1 else B("xscr")

              def setup_early(l):
                  dma(pgT[:], pgT_d[l], [], [B("pgT")], B("pgT"))
                  dma(ang[:], ang_d[l], [], [B("ang")], B("ang"))
                  dma(asb[:], asb_d[l], [], [B("asb")], B("asb"))
                  dma(bfb[:], bfb_d[l], [], [B("bfb")], B("bfb"))
                  ci = 0
                  for kc in range(8):
                      for c0 in range(0, DIN, 1024):
                          c1 = min(DIN, c0 + 1024)
                          stg, stgB = next_xa()
                          dma(stg[:, 0:c1 - c0], win_d[l, kc * 128:(kc + 1) * 128, c0:c1], [], [stgB], stgB)
                          if ci % 2 == 0:
                              ts("dve", Wb[:, kc, c0:c1], stg[:, 0:c1 - c0], pgT[:, kc:kc + 1], None, ALU.mult, None,
                                 [stgB, B("pgT")], [B("Wb")])
                          else:
                              act(Wb[:, kc, c0:c1], stg[:, 0:c1 - c0], AF.Copy, [stgB, B("pgT")], [B("Wb")], scale=pgT[:, kc:kc + 1])
                          ci += 1
                  stg, stgB = next_xa()
                  stg4 = stg[:, 0:512].rearrange("p (g j) -> p g j", g=4)
                  dma(stg4, asw_d[l].rearrange("g i j -> i g j"), [], [stgB], stgB)
                  pm4 = pmx[:, 0:512].rearrange("p (g i) -> p g i", g=4)
                  tr([(pm4[:, g, :], stg4[:, g, :], ident_f) for g in range(4)], [stgB, B("cstf")], [B("pmx")])
                  stg2, stg2B = next_xa()
                  dma(stg2[:, 0:128], smask_d[:, :], [], [stg2B], stg2B)
                  tt("dve", WsT[:], pm4, stg2[:, 0:128].unsqueeze(1).broadcast_to([128, 4, 128]), ALU.mult,
                     [B("pmx"), stg2B], [B("WsT")])

              def setup_late(l):
                  dma(postg[:], postg_d[l], [], [B("postg")], B("postg"))
                  for kc in range(8):
                      stg, stgB = next_xa()
                      dma(stg[:], wout_d[l, kc * 128:(kc + 1) * 128, :], [], [stgB], stgB)
                      cp(["dve", "act"][kc % 2], Wo[:, kc, :], stg[:], [stgB], [B("Wo")])

              if l == 0:
                  setup_early(0)
                  setup_late(0)
              stage(2)
              stage(3)
              op("pool", lambda E: E.memset(Sst[:], 0.0), [], [B("Sst")])
              op("pool", lambda E: E.memset(Sbf[:], 0.0), [], [B("Sbf")])

              def pre(t):
                  tl = t % 4
                  xt, xtB = next_xa()
                  dma(xt[:], src_d[t * 128:(t + 1) * 128, :], [srcB], [xtB], xtB)
                  cs_t, csB = cst[t % 2], B(f"cs{t % 2}")
                  dma(cs_t[:], cs_d[t * 128:(t + 1) * 128, :], [], [csB], csB)
                  act(xn[:], xt[:], AF.Square, [xtB], [B("xn"), B("st_ss")], accum=st[:, 0:1])
                  rstd_small(st[:, 0:1], st[:, 1:2], B("st_ss"), scale=1.0 / D)
                  ts("dve", xn[:], xt[:], st[:, 1:2], None, ALU.mult, None, [xtB, B("st_ss")], [B("xn")])
                  tr([(ptr[:, kc, :], xn[:, kc * 128:(kc + 1) * 128], ident_b) for kc in range(8)],
                     [B("xn"), B("cstb")], [B("ptr")])
                  hB = B(f"hT{t % 2}")
                  cp("dve", hT2[t % 2][:], ptr[:], [B("ptr")], [hB])


              def chA(t):
                  tl = t % 4
                  hB = B(f"hT{t % 2}")
                  Ib = (t // 4) % 2
                  cs_t, csB = cst[t % 2], B(f"cs{t % 2}")
                  pin_i = [0]

                  def inproj(c0, n):
                      k = 0
                      pin_i[0] += 1
                      pb, pbB = pin[k], B(f"pin{k}")
                      mm([(pb[:, 0:n], hT2[t % 2][:, kc, :], Wb[:, kc, c0:c0 + n], kc == 0, kc == 7)
                          for kc in range(8)], [hB, B("Wb")], [pbB])
                      return pb, pbB

                  pa, paB = inproj(C_AU, 512)
                  cA, cAB = sc[0], B("sc0")
                  eA, eAB = sc[1], B("sc1")
                  guv, guvB = sc[2], B("sc2")
                  act(cA[:], pa[:], AF.Square, [paB], [cAB])
                  ts("pool", cA[:], cA[:], 0.044715, 1.0, ALU.mult, ALU.add, [cAB], [cAB])
                  tt("dve", cA[:], cA[:], pa[:], ALU.mult, [cAB, paB], [cAB])
                  act(eA[:], cA[:], AF.Exp, [cAB], [eAB], scale=-2.0 * math.sqrt(2.0 / math.pi))
                  act(eA[:], eA[:], AF.Ln, [eAB], [eAB], bias=1.0)
                  act(eA[:], eA[:], AF.Exp, [eAB], [eAB], scale=-1.0)
                  tt("dve", guv[:], pa[:], eA[:], ALU.mult, [paB, eAB], [guvB])
                  gv = guv[:, 256:512]
                  gv3 = gv.rearrange("p (g c) -> p g c", g=4)
                  sq3 = cA[:, 0:256].rearrange("p (g c) -> p g c", g=4)
                  stA = B("st_A")
                  op("dve", lambda E: E.tensor_reduce(out=st[:, 8:12], in_=gv3, axis=AX.X, op=ALU.add), [guvB], [stA])
                  tt("pool", cA[:, 0:256], gv, gv, ALU.mult, [guvB], [cAB])
                  op("dve", lambda E: E.tensor_reduce(out=st[:, 12:16], in_=sq3, axis=AX.X, op=ALU.add), [cAB], [stA])
                  ts("dve", st[:, 8:12], st[:, 8:12], 1.0 / 64, None, ALU.mult, None, [stA], [stA])
                  tt("dve", st[:, 16:20], st[:, 8:12], st[:, 8:12], ALU.mult, [stA], [stA])
                  stt("dve", st[:, 12:16], st[:, 12:16], 1.0 / 64, st[:, 16:20], ALU.mult, ALU.subtract, [stA], [stA])
                  rstd_small(st[:, 12:16], st[:, 12:16], stA)
                  vc, vcB = sc[3][:, 0:256], B("sc3a")
                  vc3 = vc.rearrange("p (g c) -> p g c", g=4)
                  tt("dve", vc3, gv3, st[:, 8:12].unsqueeze(2).broadcast_to([128, 4, 64]), ALU.subtract, [guvB, stA], [vcB])
                  tt("dve", vc3, vc3, st[:, 12:16].unsqueeze(2).broadcast_to([128, 4, 64]), ALU.mult, [vcB, stA], [vcB])
                  tt("pool", vln[:], vc, ang[:], ALU.mult, [vcB, B("ang")], [B("vln")])
                  pm = pmx[:, 0:256]
                  pm3 = pm.rearrange("p (g c) -> p g c", g=4)
                  mm([(pmx[:, g * 64:(g + 1) * 64], WsT[:, g, :], vln[:, g * 64:(g + 1) * 64], True, True) for g in range(4)],
                     [B("WsT"), B("vln")], [B("pmx")])
                  mb, mbB = sc[3][:, 256:512], B("sc3b")
                  mb3 = mb.rearrange("p (g c) -> p g c", g=4)
                  tt("dve", mb3, pm3, asb[:, 0:4].unsqueeze(2).broadcast_to([128, 4, 64]), ALU.add, [B("pmx"), B("asb")], [mbB])
                  tt("pool", mb, mb, guv[:, 0:256], ALU.mult, [mbB, guvB], [mbB])
                  pg_, pgB = inproj(C_AG, 256)
                  e2, e2B = sc[1][:, 0:256], B("sc1")
                  silu_chain(pg_[:, 0:256], pgB, e2, e2B, e2, e2B)
                  tt("pool", y2[Ib][:, tl, 0:256], mb, e2, ALU.mult, [mbB, e2B], [B(f"yA{Ib}_{tl}")])


              def chR(t):
                  tl = t % 4
                  hB = B(f"hT{t % 2}")
                  Ib = (t // 4) % 2
                  cs_t, csB = cst[t % 2], B(f"cs{t % 2}")
                  pin_i = [0]

                  def inproj(c0, n):
                      k = 1
                      pin_i[0] += 1
                      pb, pbB = pin[k], B(f"pin{k}")
                      mm([(pb[:, 0:n], hT2[t % 2][:, kc, :], Wb[:, kc, c0:c0 + n], kc == 0, kc == 7)
                          for kc in range(8)], [hB, B("Wb")], [pbB])
                      return pb, pbB

                  cos2 = cs_t[:, 0:64].unsqueeze(1).broadcast_to([128, 6, 64])
                  sin_b = cs_t[:, 64:96].unsqueeze(1).broadcast_to([128, 6, 32])
                  nsin_b = cs_t[:, 96:128].unsqueeze(1).broadcast_to([128, 6, 32])
                  tA, tAB = rsA, B("rsA")
                  tB_, tBB = rsB, B("rsB")
                  for (c0, dst, dstB) in ((C_RQ, qr, B("qr")), (C_RK, kr, B("kr"))):
                      pq, pqB = inproj(c0, 384)
                      pq3 = pq[:, 0:384].rearrange("p (h c) -> p h c", h=6)
                      pq4 = pq[:, 0:384].rearrange("p (h s f) -> p h s f", h=6, s=2)
                      tA3 = tA[:, 0:384].rearrange("p (h c) -> p h c", h=6)
                      tB4 = tB_[:, 0:384].rearrange("p (h s f) -> p h s f", h=6, s=2)
                      tt("dve", tA3, pq3, cos2, ALU.mult, [pqB, csB], [tAB])
                      tt("dve", tB4[:, :, 0, :], pq4[:, :, 1, :], nsin_b, ALU.mult, [pqB, csB], [tBB])
                      tt("dve", tB4[:, :, 1, :], pq4[:, :, 0, :], sin_b, ALU.mult, [pqB, csB], [tBB])
                      tt("dve", dst[:], tA[:, 0:384], tB_[:, 0:384], ALU.add, [tAB, tBB], [dstB])
                  tt("pool", kd[:].rearrange("p (h c) -> p h c", h=6), kr[:].rearrange("p (h c) -> p h c", h=6),
                     kdec.unsqueeze(2).broadcast_to([128, 6, 64]), ALU.mult, [B("kr"), B("cstf")], [B("kd")])
                  tr([(pso1_b[:, p, :], qr[:, p * 128:(p + 1) * 128], ident_b) for p in range(3)]
                     + [(pso1_b[:, 3 + p, :], kr[:, p * 128:(p + 1) * 128], ident_b) for p in range(3)],
                     [B("qr"), B("kr"), B("cstb")], [B("pso1")])
                  cp("dve", qTz[0:64, 0:6:2, :], pso1_b[0:64, 0:3, :], [B("pso1")], [B("qTz")])
                  cp("dve", qTz[64:128, 1:6:2, :], pso1_b[64:128, 0:3, :], [B("pso1")], [B("qTz")])
                  tt("dve", qdT[:], pso1_b[:, 0:3, :], qdec, ALU.mult, [B("pso1"), B("cstf")], [B("qdT")])
                  cp("dve", kT[:], pso1_b[:, 3:6, :], [B("pso1")], [B("kT")])
                  pv, pvB = inproj(C_RV, 384)
                  cp("act", vr[:], pv[:, 0:384], [pvB], [B("vr")])
                  pgr, pgrB = inproj(C_RG, 384)
                  eR, eRB = rsE[:, 0:384], B("rsE")
                  silu_chain(pgr[:, 0:384], pgrB, eR, eRB, eR, eRB, ())
                  specs0, specs1 = [], []
                  for h in range(6):
                      p_, s_ = h // 2, h % 2
                      rows = slice(s_ * 64, (s_ + 1) * 64)
                      if s_ == 0:
                          specs0.append((pso[1][:].rearrange("p a b -> p (a b)")[:, p_ * 128:(p_ + 1) * 128], kT[:, p_, :], qTz[:, h, :], True, True))
                      else:
                          specs1.append((pin[1][:, p_ * 128:(p_ + 1) * 128], kT[:, p_, :], qTz[:, h, :], True, True))
                  mm(specs0, [B("kT"), B("qTz")], [B("pso1")])
                  mm(specs1, [B("kT"), B("qTz")], [B("pin1")])
                  tt("dve", PsT[:, 0:6:2, :], pso[1][:, 0:3, :], rmask[:, 0:6:2, :], ALU.mult,
                     [B("pso1"), B("cstb")], [B("PsT0")])
                  tt("dve", PsT[:, 1:6:2, :], pin[1][:, 0:384].rearrange("p (h i) -> p h i", h=3), rmask[:, 1:6:2, :], ALU.mult,
                     [B("pin1"), B("cstb")], [B("PsT1")])
                  pro = pso[1][:].rearrange("p a b -> p (a b)")
                  specs = []
                  for h in range(6):
                      p_, s_ = h // 2, h % 2
                      rows = slice(s_ * 64, (s_ + 1) * 64)
                      specs.append((pro[:, h * 64:(h + 1) * 64], PsT[:, h, :], vr[:, h * 64:(h + 1) * 64], True, False))
                      specs.append((pro[:, h * 64:(h + 1) * 64], qdT[rows, p_, :], Sbf[rows, p_, :], False, True))
                  mm(specs, [B("PsT0"), B("PsT1"), B("vr"), B("qdT"), B("Sbf")], [B("pso1")])
                  pkv = pin[1][:, 0:384].rearrange("p (a b) -> p a b", a=3)
                  mm([(pin[1][:, p_ * 128:(p_ + 1) * 128], kd[:, p_ * 128:(p_ + 1) * 128], vr[:, p_ * 128:(p_ + 1) * 128], True, True) for p_ in range(3)],
                     [B("kd"), B("vr")], [B("pin1")])
                  tt("pool", Sst[:], Sst[:], cd2t, ALU.mult, [B("Sst"), B("cstf")], [B("Sst")])
                  tt("dve", Sst[0:64], Sst[0:64], pkv[0:64, :, 0:64], ALU.add, [B("Sst"), B("pin1")], [B("Sst")])
                  tt("dve", Sst[64:128], Sst[64:128], pkv[64:128, :, 64:128], ALU.add, [B("Sst"), B("pin1")], [B("Sst")])
                  cp("pool", Sbf[:], Sst[:], [B("Sst")], [B("Sbf")])
                  ro3 = pro[:, 0:384].rearrange("p (h c) -> p h c", h=6)
                  stR = B("st_R")
                  sqR, sqRB = rsA, B("rsA")
                  oc, ocB = rsB, B("rsB")
                  oc3 = oc[:, 0:384].rearrange("p (h c) -> p h c", h=6)
                  op("dve", lambda E: E.tensor_reduce(out=st[:, 24:30], in_=ro3, axis=AX.X, op=ALU.add), [B("pso1")], [stR])
                  act(sqR[:, 0:384], pro[:, 0:384], AF.Square, [B("pso1")], [sqRB])
                  op("dve", lambda E: E.tensor_reduce(out=st[:, 30:36], in_=sqR[:, 0:384].rearrange("p (h c) -> p h c", h=6),
                                                      axis=AX.X, op=ALU.add), [sqRB], [stR])
                  ts("dve", st[:, 24:30], st[:, 24:30], 1.0 / 64, None, ALU.mult, None, [stR], [stR])
                  tt("dve", st[:, 36:42], st[:, 24:30], st[:, 24:30], ALU.mult, [stR], [stR])
                  stt("dve", st[:, 30:36], st[:, 30:36], 1.0 / 64, st[:, 36:42], ALU.mult, ALU.subtract, [stR], [stR])
                  rstd_small(st[:, 30:36], st[:, 30:36], stR)
                  tt("dve", oc3, ro3, st[:, 24:30].unsqueeze(2).broadcast_to([128, 6, 64]), ALU.subtract, [B("pso1"), stR], [ocB])
                  tt("dve", oc3, oc3, st[:, 30:36].unsqueeze(2).broadcast_to([128, 6, 64]), ALU.mult, [ocB, stR], [ocB])
                  tt("pool", y2[Ib][:, tl, 256:640], oc[:, 0:384], eR, ALU.mult, [ocB, eRB], [B(f"yR{Ib}_{tl}")])


              def chF(t):
                  tl = t % 4
                  hB = B(f"hT{t % 2}")
                  Ib = (t // 4) % 2
                  cs_t, csB = cst[t % 2], B(f"cs{t % 2}")
                  pin_i = [0]

                  def inproj(c0, n):
                      pin_i[0] += 1
                      pb, pbB = ptr_f, B("ptr")
                      mm([(pb[:, 0:n], hT2[t % 2][:, kc, :], Wb[:, kc, c0:c0 + n], kc == 0, kc == 7)
                          for kc in range(8)], [hB, B("Wb")], [pbB])
                      return pb, pbB

                  pfv, pfvB = inproj(C_FV, 384)
                  cp("act", V[:, t, :, 0:64], pfv[:, 0:384].rearrange("p (h c) -> p h c", h=6), [pfvB], [B(f"V{t}")])
                  pfg, pfgB = inproj(C_FG, 390)
                  eF, eFB = xn_f[:, 0:384], B("xn")
                  silu_chain(pfg[:, 0:384], pfgB, eF, eFB, sgf2[Ib][:, tl, :], B(f"sgf{Ib}_{tl}"))
                  stF = B("st_F")
                  tt("dve", st[:, 44:50], pfg[:, 384:390], bfb[:], ALU.add, [pfgB, B("bfb")], [stF])
                  act(st[:, 44:50], st[:, 44:50], AF.Exp, [stF], [stF], scale=-1.0)
                  act(st[:, 44:50], st[:, 44:50], AF.Ln, [stF], [stF], bias=1.0)
                  if t == 0:
                      mm([(pmx[:, 256:262], tri, st[:, 44:50], True, True)], [B("cstf"), stF], [B("pmx")])
                  else:
                      mm([(pmx[:, 256:262], tri, st[:, 44:50], True, False),
                          (pmx[:, 256:262], sel, ncum[:, t - 1, :], False, True)],
                         [B("cstf"), stF, B(f"ncum{t - 1}")], [B("pmx")])
                  cp("dve", ncum[:, t, :], pmx[:, 256:262], [B("pmx")], [B(f"ncum{t}")])
                  for (cb_, isq) in ((C_FQ, True), (C_FK, False)):
                      specs = []
                      for p_ in range(3):
                          for kc in range(8):
                              specs.append((ptr_f[:, p_ * 128:(p_ + 1) * 128], Wb[:, kc, cb_ + p_ * 128:cb_ + (p_ + 1) * 128],
                                            hT2[t % 2][:, kc, :], kc == 0, kc == 7))
                      mm(specs, [hB, B("Wb")], [B("ptr")])
                      src3 = ptr_f[:, 0:384].rearrange("p (a b) -> p a b", a=3)
                      if isq:
                          cp("act", qT2b[Ib][:, :, tl * 128:(tl + 1) * 128], src3, [B("ptr")], [B(f"qT2_{Ib}")])
                      else:
                          cp("dve", kT2[:, :, t * 128:(t + 1) * 128], src3, [B("ptr")], [B(f"kT2_{t}")])

              def FoX(I):
                  Ib = I % 2
                  nj = 4 * I + 4
                  mm([(pss[0][:, 0:6], sel, ncum[:, 4 * I + 1, :], True, True)], [B("cstf"), B(f"ncum{4 * I + 1}")], [B("pss0")])
                  tt("dve", nbI[:, 0:nj, :], ncum[:, 0:nj, :], pss[0][:, 0:6].unsqueeze(1).broadcast_to([128, nj, 6]), ALU.subtract,
                     [B(f"ncum{j}") for j in range(nj)] + [B("pss0")], [B("nbI")])
                  steps = [(h, j) for h in range(6) for j in range(nj)]

                  def fox_S(idx):
                      h, j = steps[idx]
                      p_, s_ = h // 2, h % 2
                      rows = slice(s_ * 64, (s_ + 1) * 64)
                      a = j - 4 * I
                      lo = max(a, 0) * 128
                      psb, psB = pss[idx % 2], B(f"pss{idx % 2}")
                      kslc = kT2[:, p_, j * 128:(j + 1) * 128]
                      if s_ == 0 and j == 0:
                          cp("dve", qz[0:64, 0, :], qT2b[Ib][0:64, p_, :], [B(f"qT2_{Ib}")], [B("qz")])
                          cp("dve", qz[64:128, 1, :], qT2b[Ib][64:128, p_, :], [B(f"qT2_{Ib}")], [B("qz")])
                      if a < 0:
                          specs = [(psb[:, 0:512], kslc, qz[:, s_, 0:512], True, True)]
                      else:
                          specs = [(psb[:, lo:lo + 128], ident_b, negm, True, False),
                                   (psb[:, lo:lo + 128], kslc, qz[:, s_, lo:lo + 128], False, True)]
                          if lo + 128 < 512:
                              specs.append((psb[:, lo + 128:512], kslc, qz[:, s_, lo + 128:512], True, True))
                      mm(specs, [B(f"kT2_{j}"), B("qz"), B("cstb")], [psB])

                  def fox_PV(idx):
                      h, j = steps[idx]
                      a = j - 4 * I
                      lo = max(a, 0) * 128
                      po, poB = pso[0], B("pso0")
                      psb, psB = pss[idx % 2], B(f"pss{idx % 2}")
                      ptb, ptB = PT[idx % NPT], B(f"PT{idx % NPT}")
                      act(ptb[:, lo:512], psb[:, lo:512], AF.Exp, [psB, B("nbI")], [ptB], bias=nbI[:, j, h:h + 1], scale=0.125)
                      specs = []
                      for qt in range(lo // 128, 4):
                          specs.append((po[:, qt, 0:65], ptb[:, qt * 128:(qt + 1) * 128], V[:, j, h, 0:65], j == 0 and qt == 0, j == 4 * I + qt, True))
                      mm(specs, [ptB, B(f"V{j}"), B("Vones")], [poB])
                      if j == nj - 1:
                          op("dve", lambda E, po=po: E.reciprocal(out=st[:, 52:56], in_=po[:, :, 64:65].rearrange("p a b -> p (a b)")),
                             [poB], [B("st_O")])
                          for qt in range(4):
                              stt("dve", y2[Ib][:, qt, 640 + h * 64:640 + (h + 1) * 64], po[:, qt, 0:64], st[:, 52 + qt:53 + qt],
                                  sgf2[Ib][:, qt, h * 64:(h + 1) * 64], ALU.mult, ALU.mult,
                                  [poB, B("st_O"), B(f"sgf{Ib}_{qt}")], [B(f"yF{Ib}_{h}")])

                  fox_S(0)
                  for idx in range(len(steps)):
                      if idx + 1 < len(steps):
                          fox_S(idx + 1)
                      fox_PV(idx)

              def P3t(t):
                  tl = t % 4
                  Ib = (t // 4) % 2
                  yreads = [B(f"yA{Ib}_{tl}"), B(f"yR{Ib}_{tl}")] + [B(f"yF{Ib}_{h}") for h in range(6)]
                  tr([(ptr[:, kc, :], y2[Ib][:, tl, kc * 128:(kc + 1) * 128], ident_b) for kc in range(8)], yreads + [B("cstb")], [B("ptr")])
                  yTt, yTB = hT2[1], B("hT1")
                  cp("dve", yTt[:], ptr[:], [B("ptr")], [yTB])
                  mm([(pin[hf][:], yTt[:, kc, :], Wo[:, kc, hf * 512:(hf + 1) * 512], kc == 0, kc == 7)
                      for kc in range(8) for hf in range(2)], [yTB, B("Wo")], [B("pin0"), B("pin1")])
                  xr, xrB = next_xa()
                  dma(xr[:], src_d[t * 128:(t + 1) * 128, :], [srcB], [xrB], xrB)
                  stP = B("st_P")
                  junk, junkB = sc[3], B("sc3")
                  tmps = [(sc[1], B("sc1")), (sc[2], B("sc2"))]
                  for hf in range(2):
                      act(junk[:], pin[hf][:], AF.Square, [B(f"pin{hf}")], [junkB, stP], accum=st[:, 58 + hf:59 + hf])
                      tt("dve", tmps[hf][0][:], pin[hf][:], postg[:, hf * 512:(hf + 1) * 512], ALU.mult,
                         [B(f"pin{hf}"), B("postg")], [tmps[hf][1]])
                  tt("dve", st[:, 60:61], st[:, 58:59], st[:, 59:60], ALU.add, [stP], [stP])
                  rstd_small(st[:, 60:61], st[:, 60:61], stP, scale=1.0 / D)
                  for hf in range(2):
                      stt("dve", xr[:, hf * 512:(hf + 1) * 512], tmps[hf][0][:], st[:, 60:61], xr[:, hf * 512:(hf + 1) * 512],
                          ALU.mult, ALU.add, [tmps[hf][1], stP, xrB], [xrB])
                  dma(dst_d[t * 128:(t + 1) * 128, :], xr[:], [xrB], [dstB], xrB)

              def P1(I):
                  if I == 0:
                      pre(0)
                  for tl_ in range(4):
                      t_ = 4 * I + tl_

                      def c_a(t_=t_):
                          chA(t_)

                      def c_r(t_=t_):
                          chR(t_)

                      def c_fp(t_=t_, tl_=tl_):
                          chF(t_)
                          if t_ + 1 < NT:
                              pre(t_ + 1)

                      Sd.run_interleaved([c_a, c_r, c_fp])

              def P3(I):
                  for tl_ in range(4):
                      P3t(4 * I + tl_)

              stage(8)
              P1(0)
              for I in range(NS):
                  def side_x(I=I):
                      FoX(I)

                  def side_y(I=I):
                      if I > 0:
                          P3(I - 1)
                      if I + 1 < NS:
                          P1(I + 1)
                      elif l + 1 < L:
                          setup_early(l + 1)

                  Sd.run_interleaved([side_x, side_y])
              P3(NS - 1)
              if l + 1 < L:
                  setup_late(l + 1)

        except _Stop:
            pass
        Sd.finish("sp")
        Sd.emit(block)
    return nc


def _consts(S):
    cf = np.zeros((128, NCF), np.float64)
    idx = np.arange(128)
    cf[:, K_ID:K_ID + 128] = np.eye(128)
    cf[:, K_TRI:K_TRI + 128] = (idx[:, None] <= idx[None, :])
    cf[:, K_SEL:K_SEL + 128] = (idx[:, None] == 127)
    sm = ((idx[:, None] // 64) <= (idx[None, :] // 64)).astype(np.float32)
    gam = 1.0 - np.exp2(-5.0 - np.arange(6))
    lg = np.log(gam)
    for p in range(3):
        for s in range(2):
            h = 2 * p + s
            cf[s * 64:(s + 1) * 64, K_QD + p * 128:K_QD + (p + 1) * 128] = np.exp(lg[h] * (idx[None, :] + 1.0)) * 0.125
            cf[s * 64:(s + 1) * 64, K_CD + p * 64:K_CD + (p + 1) * 64] = np.exp(lg[h] * 128.0)
    for h in range(6):
        cf[:, K_KD + h] = np.exp(lg[h] * (127.0 - idx))
    cb = np.zeros((128, NCB), np.float64)
    cb[:, KB_ID:KB_ID + 128] = np.eye(128)
    cb[:, KB_NEG:KB_NEG + 128] = np.where(idx[:, None] <= idx[None, :], 0.0, -30000.0)
    j = idx[:, None]
    i = idx[None, :]
    same = (j // 64) == (i // 64)
    fwd = (j // 64 == 0) & (i // 64 == 1)
    for h in range(6):
        m = np.where(same, np.exp(lg[h] * np.abs(i - j)), np.where(fwd, np.exp(lg[h] * (i - j)), 0.0)) * 0.125
        cb[:, KB_RM + h * 128:KB_RM + (h + 1) * 128] = m
    half = 32
    inv = 10000.0 ** (-np.arange(half, dtype=np.float64) / half)
    ang = np.arange(S, dtype=np.float64)[:, None] * inv[None, :]
    cs = np.concatenate([np.cos(ang), np.cos(ang), np.sin(ang), -np.sin(ang)], axis=1)
    return cf.astype(np.float32), cb.astype(ml_dtypes.bfloat16), cs.astype(np.float32), sm


_CACHE = {}


def run(x, pre_gain, post_gain, w_in, b_forget, a_norm_gain, a_spatial_w, a_spatial_b, w_out):
    x = np.asarray(x, np.float32)
    Bn, S, _ = x.shape
    L = w_in.shape[0]
    key = (S, L)
    if key not in _CACHE:
        _CACHE[key] = build_program(S, L)
    nc = _CACHE[key]
    cf, cb, cs, sm = _consts(S)
    f = lambda a: np.ascontiguousarray(np.asarray(a, np.float32))
    shared = {
        "w_in": f(w_in), "w_out": f(w_out),
        "pgT": f(np.asarray(pre_gain).reshape(L, 8, 128).transpose(0, 2, 1)),
        "postg": f(np.broadcast_to(np.asarray(post_gain)[:, None, :], (L, 128, D))),
        "ang": f(np.broadcast_to(np.asarray(a_norm_gain).reshape(L, 1, 256), (L, 128, 256))),
        "asw": f(a_spatial_w),
        "asb": f(np.asarray(a_spatial_b).transpose(0, 2, 1)),
        "bfb": f(np.broadcast_to(np.asarray(b_forget)[:, None, :], (L, 128, 6))),
        "cstf": cf, "cstb": cb, "cs": cs, "smask": sm,
    }
    in_maps = []
    for b in range(Bn):
        m = dict(shared)
        m["x"] = np.ascontiguousarray(x[b])
        in_maps.append(m)
    res = run_bass_kernel_spmd(nc, in_maps, core_ids=list(range(Bn)))
    return np.stack([np.asarray(r["out"], np.float32) for r in res.results], axis=0)


def kernel(x, pre_gain, post_gain, w_in, b_forget, a_norm_gain, a_spatial_w, a_spatial_b, w_out):
    return run(x, pre_gain, post_gain, w_in, b_forget, a_norm_gain, a_spatial_w, a_spatial_b, w_out)
```
